# Optimizing a Trainium2 kernel written in Bass

```python
import math
import jax, jax.numpy as jnp
from jax import lax
import numpy as np

D_MODEL = 4096
BATCH = 32
SEQ = 256
DEPTH = 2
DEC_BATCH = 2
DEC_SEQ = 2048
PAST_LEN = 512

GRID_W = 64
N_MIXERS = 2
N_POOL_LAYERS = (DEPTH + 1) // 2
N_SSD_LAYERS = DEPTH // 2
E_POOL = 2 * D_MODEL
POOL_WINDOWS = (2, 4, 8, 16)
N_POOL_GROUPS = len(POOL_WINDOWS)
POOL_GROUP_W = E_POOL // N_POOL_GROUPS
D_INNER = 2 * D_MODEL
HEAD_DIM = 64
N_HEADS = D_INNER // HEAD_DIM
D_STATE = 128
N_GROUPS = 8
HEADS_PER_GROUP = N_HEADS // N_GROUPS
CONV_CH = D_INNER + 2 * N_GROUPS * D_STATE
CONV_W = 7
CHUNK = 128
SSD_IN = D_INNER + CONV_CH + 2 * N_HEADS
EPS = 1e-6

kernel_name = 'hybrid_pool_ssd_diffusion_step'


def _silu(t):
    return t * jax.nn.sigmoid(t)


def _rmsnorm(t, w):
    t32 = t.astype(jnp.float32)
    t32 = t32 * lax.rsqrt(jnp.mean(t32 * t32, axis=-1, keepdims=True) + EPS)
    return (t32 * w.astype(jnp.float32)).astype(t.dtype)


def _bounds(n, w):
    t = np.arange(n)
    lo = np.clip(t - w // 2, 0, n)
    hi = np.clip(t - w // 2 + w, 0, n)
    return lo, hi, (hi - lo).astype(np.float32)


def _pool1d(x, w):
    b, L, C = x.shape
    x32 = x.astype(jnp.float32)
    S = jnp.concatenate([jnp.zeros((b, 1, C), jnp.float32), jnp.cumsum(x32, axis=1)], axis=1)
    lo, hi, cnt = _bounds(L, w)
    return ((S[:, hi] - S[:, lo]) / cnt[None, :, None]).astype(x.dtype)


def _pool2d(x, w):
    b, L, C = x.shape
    rows = L // GRID_W
    xr = x.astype(jnp.float32).reshape(b, rows, GRID_W, C)
    S = jnp.cumsum(jnp.cumsum(xr, axis=1), axis=2)
    S = jnp.pad(S, ((0, 0), (1, 0), (1, 0), (0, 0)))
    rlo, rhi, rc = _bounds(rows, w)
    clo, chi, cc = _bounds(GRID_W, w)
    s_hi = S[:, rhi]
    s_lo = S[:, rlo]
    tot = s_hi[:, :, chi] - s_hi[:, :, clo] - s_lo[:, :, chi] + s_lo[:, :, clo]
    cnt = rc[:, None] * cc[None, :]
    return (tot / cnt[None, :, :, None]).reshape(b, L, C).astype(x.dtype)


def _pool_branch(u, in_w, grp_w, grp_b, scale, out_w, grid):
    b, L, _ = u.shape
    proj = u @ in_w
    xp, z = proj[..., :E_POOL], proj[..., E_POOL:]
    xg = xp.reshape(b, L, N_POOL_GROUPS, POOL_GROUP_W)
    pool = _pool2d if grid else _pool1d
    mixed = jnp.stack([pool(xg[:, :, g], w) - xg[:, :, g] for g, w in enumerate(POOL_WINDOWS)], axis=2)
    mixed = jnp.einsum('blgc,gcd->blgd', mixed, grp_w) + grp_b
    mixed = mixed.reshape(b, L, E_POOL) * scale
    return (mixed * _silu(z)) @ out_w


def _dwconv(u, w, bias):
    out = lax.conv_general_dilated(u, w[:, None, :], window_strides=(1,),
                                   padding=[(CONV_W // 2, CONV_W // 2)],
                                   dimension_numbers=('NWC', 'WIO', 'NWC'),
                                   feature_group_count=u.shape[-1])
    return out + bias


def _ssd_scan(x, dt, A, B, C, h0):
    b, L, H, P = x.shape
    nc = L // CHUNK
    f32 = jnp.float32
    G, R, N = N_GROUPS, HEADS_PER_GROUP, D_STATE
    xdt = (x.astype(f32) * dt[..., None]).reshape(b, nc, CHUNK, G, R, P)
    a_cum = jnp.cumsum((dt * A).reshape(b, nc, CHUNK, G, R), axis=2)
    Bc = B.astype(f32).reshape(b, nc, CHUNK, G, N)
    Cc = C.astype(f32).reshape(b, nc, CHUNK, G, N)
    at = jnp.moveaxis(a_cum, 2, -1)
    diff = at[..., :, None] - at[..., None, :]
    mask = np.tril(np.ones((CHUNK, CHUNK), dtype=bool))
    decay = jnp.exp(jnp.where(mask, diff, -jnp.inf))
    scores = jnp.einsum('bclgn,bcsgn->bcgls', Cc, Bc)
    y_diag = jnp.einsum('bcgrls,bcsgrp->bclgrp', scores[:, :, :, None] * decay, xdt)
    decay_s = jnp.exp(a_cum[:, :, -1:] - a_cum)
    states = jnp.einsum('bclgn,bclgrp->bcgrpn', Bc, xdt * decay_s[..., None])
    chunk_decay = jnp.exp(a_cum[:, :, -1])

    def step(h, inp):
        s, d = inp
        return h * d[..., None, None] + s, h

    h_init = h0.astype(f32).reshape(b, G, R, P, N)
    h_fin, h_in = lax.scan(step, h_init, (jnp.moveaxis(states, 1, 0), jnp.moveaxis(chunk_decay, 1, 0)))
    h_in = jnp.moveaxis(h_in, 0, 1)
    y_off = jnp.einsum('bclgn,bcgrpn->bclgrp', Cc, h_in) * jnp.exp(a_cum)[..., None]
    y = (y_diag + y_off).reshape(b, L, H, P).astype(x.dtype)
    return y, h_fin.reshape(b, H, P, N).astype(h0.dtype)


def _ssd_branch(u, in_w, conv_w, conv_b, dt_bias, A_log, d_skip, gnorm_w, out_w, h0):
    b, L, _ = u.shape
    proj = u @ in_w
    z = proj[..., :D_INNER]
    xbc = _silu(_dwconv(proj[..., D_INNER:D_INNER + CONV_CH], conv_w, conv_b))
    dt_raw = proj[..., D_INNER + CONV_CH:]
    xs = xbc[..., :D_INNER].reshape(b, L, N_HEADS, HEAD_DIM)
    Bm = xbc[..., D_INNER:D_INNER + N_GROUPS * D_STATE].reshape(b, L, N_GROUPS, D_STATE)
    Cm = xbc[..., D_INNER + N_GROUPS * D_STATE:].reshape(b, L, N_GROUPS, D_STATE)
    dt = jax.nn.softplus(dt_raw.astype(jnp.float32).reshape(b, L, 2, N_HEADS) + dt_bias.astype(jnp.float32))
    A = -jnp.exp(A_log.astype(jnp.float32))
    flip = lambda t: jnp.flip(t, axis=1)
    y_f, h_f = _ssd_scan(xs, dt[:, :, 0], A[0], Bm, Cm, h0[:, 0])
    y_b, h_b = _ssd_scan(flip(xs), flip(dt[:, :, 1]), A[1], flip(Bm), flip(Cm), h0[:, 1])
    y = y_f + flip(y_b) + xs * d_skip[:, None]
    y = _rmsnorm(y.reshape(b, L, D_INNER) * _silu(z), gnorm_w)
    return y @ out_w, jnp.stack([h_f, h_b], axis=1)


def _modulate(t, mod):
    shift, scale, gate = jnp.split(mod, 3, axis=-1)
    return t * (1 + scale) + shift, gate


def setup_inputs(seed: int = 0) -> dict:
    key = jax.random.key(seed)
    ks = jax.random.split(key, 22)
    nrm = jax.random.normal
    D = D_MODEL
    dt0 = jnp.exp(jax.random.uniform(ks[16], (N_SSD_LAYERS, 2, N_HEADS),
                                     minval=math.log(1e-3), maxval=math.log(1e-1)))
    return {
        'x_prompt': nrm(ks[0], (BATCH, SEQ, D)),
        'x_sample': nrm(ks[1], (DEC_BATCH, DEC_SEQ, D)),
        'state_ssd': 0.3 * nrm(ks[2], (DEC_BATCH, N_SSD_LAYERS, 2, N_HEADS, HEAD_DIM, D_STATE)),
        'c': nrm(ks[3], (DEC_BATCH, D)),
        'c_ctx': nrm(ks[4], (D,)),
        'ada_w': nrm(ks[5], (DEPTH, D, 3 * D)) * (0.5 * D ** -0.5),
        'ada_b': 0.02 * nrm(ks[6], (DEPTH, 3 * D)),
        'norm_w': 1.0 + 0.02 * nrm(ks[7], (DEPTH, D)),
        'pool_in_w': nrm(ks[8], (N_POOL_LAYERS, D, 2 * E_POOL)) * D ** -0.5,
        'pool_grp_w': nrm(ks[9], (N_POOL_LAYERS, N_POOL_GROUPS, POOL_GROUP_W, POOL_GROUP_W)) * POOL_GROUP_W ** -0.5,
        'pool_grp_b': 0.02 * nrm(ks[10], (N_POOL_LAYERS, N_POOL_GROUPS, POOL_GROUP_W)),
        'pool_scale': 1.0 + 0.02 * nrm(ks[11], (N_POOL_LAYERS, E_POOL)),
        'pool_out_w': nrm(ks[12], (N_POOL_LAYERS, E_POOL, D)) * E_POOL ** -0.5,
        'ssd_in_w': nrm(ks[13], (N_SSD_LAYERS, D, SSD_IN)) * D ** -0.5,
        'ssd_conv_w': nrm(ks[14], (N_SSD_LAYERS, CONV_W, CONV_CH)) * CONV_W ** -0.5,
        'ssd_conv_b': 0.02 * nrm(ks[15], (N_SSD_LAYERS, CONV_CH)),
        'ssd_dt_bias': dt0 + jnp.log(-jnp.expm1(-dt0)),
        'ssd_A_log': jnp.log(jax.random.uniform(ks[17], (N_SSD_LAYERS, 2, N_HEADS), minval=1.0, maxval=16.0)),
        'ssd_D': 1.0 + 0.1 * nrm(ks[18], (N_SSD_LAYERS, N_HEADS)),
        'ssd_norm_w': 1.0 + 0.02 * nrm(ks[19], (N_SSD_LAYERS, D_INNER)),
        'ssd_out_w': nrm(ks[20], (N_SSD_LAYERS, D_INNER, D)) * D_INNER ** -0.5,
        'final_norm_w': 1.0 + 0.02 * nrm(ks[21], (D,)),
    }


def reference(x_prompt, x_sample, state_ssd, c, c_ctx, ada_w, ada_b, norm_w,
              pool_in_w, pool_grp_w, pool_grp_b, pool_scale, pool_out_w,
              ssd_in_w, ssd_conv_w, ssd_conv_b, ssd_dt_bias, ssd_A_log, ssd_D,
              ssd_norm_w, ssd_out_w, final_norm_w):
    h_ctx, h_lat = x_prompt, x_sample
    new_states = []
    for i in range(DEPTH):
        j = i // N_MIXERS
        mod_ctx = _silu(c_ctx) @ ada_w[i] + ada_b[i]
        mod_lat = (_silu(c) @ ada_w[i] + ada_b[i])[:, None, :]
        u_ctx, g_ctx = _modulate(_rmsnorm(h_ctx, norm_w[i]), mod_ctx)
        u_lat, g_lat = _modulate(_rmsnorm(h_lat, norm_w[i]), mod_lat)
        if i % N_MIXERS == 0:
            pw = (pool_in_w[j], pool_grp_w[j], pool_grp_b[j], pool_scale[j], pool_out_w[j])
            o_ctx = _pool_branch(u_ctx, *pw, grid=False)
            o_lat = _pool_branch(u_lat, *pw, grid=True)
        else:
            sw = (ssd_in_w[j], ssd_conv_w[j], ssd_conv_b[j], ssd_dt_bias[j], ssd_A_log[j],
                  ssd_D[j], ssd_norm_w[j], ssd_out_w[j])
            h0_ctx = jnp.zeros((u_ctx.shape[0], 2, N_HEADS, HEAD_DIM, D_STATE), state_ssd.dtype)
            o_ctx, st_ctx = _ssd_branch(u_ctx, *sw, h0_ctx)
            o_lat, _ = _ssd_branch(u_lat, *sw, state_ssd[:, j])
            new_states.append(st_ctx)
        h_ctx = h_ctx + g_ctx * o_ctx
        h_lat = h_lat + g_lat * o_lat
    y_prompt = _rmsnorm(h_ctx, final_norm_w)
    y_sample = _rmsnorm(h_lat, final_norm_w)
    new_state_ssd = jnp.stack(new_states, axis=1)
    return (y_prompt, y_sample, new_state_ssd)
```

```python
from contextlib import ExitStack
import numpy as np
import concourse.bass as bass
import concourse.mybir as mybir
from concourse.bass_utils import run_bass_kernel_spmd

F32, BF16 = mybir.dt.float32, mybir.dt.bfloat16
AF = mybir.ActivationFunctionType
ALU = mybir.AluOpType
AX = mybir.AxisListType
EPS = 1e-6
WINS = (2, 4, 8, 16)
USE_WCACHE = True


class Cfg:
    def __init__(self, D=4096, NPS=4, SEQ=256, DSEQ=2048):
        self.D, self.NPS, self.SEQ, self.DSEQ, self.GW = D, NPS, SEQ, DSEQ, 64
        self.E = 2 * D
        self.PGW = self.E // 4
        self.DI = 2 * D
        self.P = 64
        self.H = self.DI // 64
        self.N = 128
        self.G = 8
        self.HPG = self.H // self.G
        self.GN = self.G * self.N
        self.CONVCH = self.DI + 2 * self.GN
        self.SSD_IN = self.DI + self.CONVCH + 2 * self.H
        self.TP = NPS * SEQ
        self.T = self.TP + DSEQ
        self.TB = 512
        self.NB = self.T // self.TB
        self.KC = D // 128


class Buf:
    __slots__ = ("w", "rs")

    def __init__(self):
        self.w = None
        self.rs = []


class Stream:
    def __init__(self, name, h, sem):
        self.name, self.h, self.sem, self.cnt, self.seen = name, h, sem, 0, {}


class Queue:
    def __init__(self, name, stream, sems):
        self.name, self.stream, self.sems, self.cnt = name, stream, sems, 0


class Sched:
    def __init__(self):
        self.S = {}
        self.Q = {}
        self.bufs = {}

    def buf(self, key):
        b = self.bufs.get(key)
        if b is None:
            b = self.bufs[key] = Buf()
        return b

    def _wait(self, st, dep):
        kind, name, idx = dep
        if kind == "c":
            if name == st.name and name == "pe":
                return
            if st.seen.get(name, 0) >= idx:
                return
            st.h.wait_ge(self.S[name].sem, idx)
            st.seen[name] = idx
        else:
            q = self.Q[name]
            ns = len(q.sems)
            s = idx % ns
            val = 16 * (idx // ns + 1)
            key = (name, s)
            if st.seen.get(key, 0) >= val:
                return
            st.h.wait_ge(q.sems[s], val)
            st.seen[key] = val

    def op(self, sname, fn, reads=(), writes=(), q=None):
        st = self.S[sname]
        deps = set()
        for b in reads:
            if b.w is not None:
                deps.add(b.w)
        for b in writes:
            if b.w is not None:
                deps.add(b.w)
            deps.update(b.rs)
        qq = None
        if q is not None:
            qq = self.Q[q]
            if qq.cnt >= len(qq.sems):
                deps.add(("d", q, qq.cnt - len(qq.sems)))
        for d in sorted(deps):
            self._wait(st, d)
        ins = fn(st.h)
        if qq is None:
            st.cnt += 1
            ins.then_inc(st.sem, 1)
            me = ("c", sname, st.cnt)
        else:
            ins.then_inc(qq.sems[qq.cnt % len(qq.sems)], 16)
            me = ("d", q, qq.cnt)
            qq.cnt += 1
        for b in reads:
            b.rs.append(me)
        for b in writes:
            b.w = me
            b.rs = []
        return me

    def barrier(self):
        deps = []
        for s in self.S.values():
            if s.cnt:
                deps.append(("c", s.name, s.cnt))
        for q in self.Q.values():
            for i in range(max(0, q.cnt - len(q.sems)), q.cnt):
                deps.append(("d", q.name, i))
        for st in self.S.values():
            for d in deps:
                if d[0] == "c" and d[1] == st.name:
                    continue
                self._wait(st, d)
        for b in self.bufs.values():
            b.w = None
            b.rs = []


def build(c):
    D, E, DI, H, N, G, P, HPG, GN = c.D, c.E, c.DI, c.H, c.N, c.G, c.P, c.HPG, c.GN
    T, TB, NB, KC, TP = c.T, c.TB, c.NB, c.KC, c.TP
    CONVCH, SSD_IN, PGW = c.CONVCH, c.SSD_IN, c.PGW
    NTT = TB // 128
    nc = bass.Bass("TRN2", target_bir_lowering=False)

    def din(name, shape):
        return nc.dram_tensor(name, list(shape), F32, kind="ExternalInput").ap()

    def dscr(name, shape, dt):
        return nc.dram_tensor(name, list(shape), dt, kind="Internal").ap()

    x_in = din("x", [T, D])
    st_in = din("st", [2, H * P, N])
    cv_in = din("cv", [2, D])
    ada_w = din("ada_w", [2, D, 3 * D])
    ada_b = din("ada_b", [2, 3 * D])
    norm_w = din("norm_w", [2, D])
    pin_w = din("pool_in_w", [D, 2 * E])
    pgrp_w = din("pool_grp_w", [4, PGW, PGW])
    pgrp_b = din("pool_grp_b", [E])
    pscale = din("pool_scale", [E])
    pout_w = din("pool_out_w", [E, D])
    sin_w = din("ssd_in_w", [D, SSD_IN])
    conv_w = din("ssd_conv_w", [7, CONVCH])
    conv_b = din("ssd_conv_b", [CONVCH])
    dt_bias = din("ssd_dt_bias", [2 * H])
    a_log = din("ssd_A_log", [2 * H])
    d_skip = din("ssd_D", [H])
    gn_w = din("ssd_norm_w", [DI])
    sout_w = din("ssd_out_w", [DI, D])
    fn_w = din("final_norm_w", [D])
    y_out = nc.dram_tensor("y", [T, D], F32, kind="ExternalOutput").ap()
    ns_out = nc.dram_tensor("ns", [c.NPS, 2, H * P, N], F32, kind="ExternalOutput").ap()

    xp_tok = dscr("xp_tok", [T, E], BF16)
    zT = dscr("zT", [E, T], BF16)
    gT = dscr("gT", [E, T], BF16)
    h1 = dscr("h1", [T, D], F32)
    h2 = dscr("h2", [T, D], F32)
    z_tok = dscr("z_tok", [T, DI], BF16)
    xbc_pre = dscr("xbc_pre", [CONVCH, T], BF16)
    xbcT = dscr("xbcT", [CONVCH, T], BF16)
    xb_tok = dscr("xb_tok", [T, DI + GN], BF16)
    dt_tok = dscr("dt_tok", [T, 2 * H], F32)
    ybd = dscr("ybd", [T, DI], BF16)
    gb_d = dscr("gb_d", [2, 2, 128, D], F32)
    NCBM = max((2 * E + 511) // 512, (SSD_IN + 511) // 512)
    wcache = dscr("wcache", [NCBM, 128, KC, 512], BF16)
    wcache2 = dscr("wcache2", [(D // 512) * (E // 128 // 16), 128, 16, 512], BF16)

    sch = Sched()
    op = sch.op
    B = sch.buf
    es = ExitStack()
    with es:
        uid = [0]

        def sb(name, shape, dt, stack=None):
            uid[0] += 1
            return (stack or es).enter_context(nc.sbuf_tensor("%s_%d" % (name, uid[0]), list(shape), dt))

        def ps(name, shape, dt, stack=None):
            uid[0] += 1
            return (stack or es).enter_context(nc.psum_tensor("%s_%d" % (name, uid[0]), list(shape), dt))

        def sem(name):
            return es.enter_context(nc.semaphore(name))

        sems_c = {n: sem("c_" + n) for n in ("pe", "act", "dve", "pool", "sp")}
        qsp = [sem("qsp%d" % i) for i in range(16)]
        qpl = [sem("qpl%d" % i) for i in range(12)]
        sch.S = {"pe": Stream("pe", nc.tensor, sems_c["pe"]),
                 "act": Stream("act", nc.scalar, sems_c["act"]),
                 "dve": Stream("dve", nc.vector, sems_c["dve"]),
                 "pool": Stream("pool", nc.gpsimd, sems_c["pool"]),
                 "sp": Stream("sp", nc.sync, sems_c["sp"])}
        sch.Q = {"q_sp": Queue("q_sp", "sp", qsp), "q_pool": Queue("q_pool", "pool", qpl)}

        def load(out, in_, reads=(), writes=()):
            return op("sp", lambda h: h.dma_start(out=out, in_=in_), reads, writes, q="q_sp")

        def loadw(out, in_, reads=(), writes=()):
            return op("pool", lambda h: h.dma_start(out=out, in_=in_), reads, writes, q="q_pool")

        def wload(wb, pbufs, W2, r0, nk, c0, w, piece=8):
            for j, k0 in enumerate(range(0, nk, piece)):
                k1 = min(nk, k0 + piece)
                loadw(wb[:, k0:k1, :w], W2[r0 + k0 * 128:r0 + k1 * 128, c0:c0 + w].rearrange("(kc p) c -> p kc c", p=128), writes=[pbufs[j]])

        identf = sb("identf", [128, 128], F32)
        identb = sb("identb", [128, 128], BF16)
        onesf = sb("onesf", [128, 128], F32)
        onesb = sb("onesb", [128, 128], BF16)
        Lf = sb("Lf", [128, 128], F32)
        Lb = sb("Lb", [128, 128], F32)
        Uf = sb("Uf", [128, 128], F32)
        Ub = sb("Ub", [128, 128], F32)
        rstdy = sb("rstdy", [128, T // 128], F32)
        NCT = 3 * D // 128
        modc = [sb("modc%d" % i, [128, 2 * KC, 2], F32) for i in range(2)]
        Acol = [sb("Acol%d" % i, [128, KC, 2], F32) for i in range(2)]
        bC = B("consts")

        def mk_consts(h):
            h.memset(identf[:], 1.0)
            h.affine_select(out=identf[:], in_=identf[:], pattern=[[1, 128]], compare_op=ALU.is_equal, fill=0.0, base=0, channel_multiplier=-1)
            h.memset(onesf[:], 1.0)
            h.memset(onesb[:], 1.0)
            h.memset(Lf[:], 1.0)
            h.affine_select(out=Lf[:], in_=Lf[:], pattern=[[1, 128]], compare_op=ALU.is_ge, fill=0.0, base=0, channel_multiplier=-1)
            h.memset(Lb[:], 1.0)
            h.affine_select(out=Lb[:], in_=Lb[:], pattern=[[-1, 128]], compare_op=ALU.is_ge, fill=0.0, base=0, channel_multiplier=1)
            h.memset(Uf[:], 1.0)
            h.affine_select(out=Uf[:], in_=Uf[:], pattern=[[-1, 128]], compare_op=ALU.is_ge, fill=0.0, base=-1, channel_multiplier=1)
            h.memset(Ub[:], 1.0)
            return h.affine_select(out=Ub[:], in_=Ub[:], pattern=[[1, 128]], compare_op=ALU.is_ge, fill=0.0, base=-1, channel_multiplier=-1)

        def init_consts():
            op("pool", mk_consts, writes=[bC])
            op("dve", lambda h: h.tensor_copy(out=identb[:], in_=identf[:]), reads=[bC], writes=[bC])

        def mk_load_cols(stack, colps):
            colstage = sb("colstage", [128, 128], F32, stack)

            def load_cols(vec_ap, out_fn, n, bout):
                for i0 in range(0, n, 128):
                    m = min(128, n - i0)
                    load(colstage[:m, :], vec_ap[i0 * 128:(i0 + m) * 128].rearrange("(n p) -> n p", p=128), writes=[B("colstage")])
                    op("pe", lambda h: h.transpose(colps[:, :m], colstage[:m, :], identf[:m, :m]), reads=[B("colstage"), bC], writes=[B("colps")])
                    op("dve", lambda h: h.tensor_copy(out=out_fn(i0, m), in_=colps[:, :m]), reads=[B("colps")], writes=[bout])
            return load_cols

        def blk_cond(b):
            return 0 if b * TB < TP else 1

        def phase_ada():
            with ExitStack() as e1:
                gps = [ps("adagps%d" % j, [128, 512], F32, e1) for j in range(2)]
                aps = [ps("adaps%d" % j, [128, 4, 2], F32, e1) for j in range(2)]
                colps = ps("colps", [128, 128], F32, e1)
                load_cols = mk_load_cols(e1, colps)
                cvc = sb("cvc", [128, 2, KC], F32, e1)
                scol = sb("scol", [128, KC, 2], BF16, e1)
                scb = sb("scb", [128, KC, 2, 128], BF16, e1)
                adab = sb("adab", [128, 2, 2 * KC], F32, e1)
                nwc = sb("nwc", [128, 2, KC], F32, e1)
                abr = sb("abr", [128, 2, D], F32, e1)
                wbuf = [sb("adaw%d" % j, [128, KC, 512], BF16, e1) for j in range(2)]
                wpb = [[B("adaw%d_%d" % (j, k)) for k in range(KC // 8 + 1)] for j in range(2)]
                gst = [sb("gst%d" % j, [128, 512], F32, e1) for j in range(2)]
                e1.enter_context(nc.Block())
                init_consts()
                bcv = B("cvc")
                load_cols(cv_in.rearrange("a d -> (a d)"), lambda i0, m: cvc[:].rearrange("p a k -> p (a k)")[:, i0:i0 + m], 2 * KC, bcv)
                op("act", lambda h: h.activation(out=scol[:], in_=cvc[:].rearrange("p a k -> p k a"), func=AF.Silu), reads=[bcv], writes=[B("scol")])
                op("dve", lambda h: h.tensor_copy(out=scb[:], in_=scol[:].unsqueeze(3).to_broadcast([128, KC, 2, 128])), reads=[B("scol")], writes=[B("scb")])
                for i in range(2):
                    load_cols(ada_b[i, 0:2 * D], lambda i0, m, i=i: adab[:, i, i0:i0 + m], 2 * KC, B("adab"))
                    load_cols(norm_w[i], lambda i0, m, i=i: nwc[:, i, i0:i0 + m], KC, B("nwc"))
                    load(abr[:, i, :], ada_b[i, 2 * D:3 * D].partition_broadcast(128), writes=[B("abr")])
                blocks = [(i, cb) for i in range(2) for cb in range(3 * D // 512)]
                wload(wbuf[0], wpb[0], ada_w[0], 0, KC, 0, 512)
                gi = 0
                for it, (i, cb) in enumerate(blocks):
                    if it + 1 < len(blocks):
                        i2, cb2 = blocks[it + 1]
                        wload(wbuf[(it + 1) % 2], wpb[(it + 1) % 2], ada_w[i2], 0, KC, cb2 * 512, 512)
                    wb, bw = wbuf[it % 2], wpb[it % 2]
                    if cb < 2 * D // 512:
                        pp, bp = aps[it % 2], B("adaps%d" % (it % 2))

                        def mm(h, wb=wb, pp=pp):
                            ins = None
                            for ct in range(4):
                                for k in range(KC):
                                    ins = h.matmul(pp[:, ct, :], lhsT=wb[:, k, ct * 128:(ct + 1) * 128], rhs=scol[:, k, :], start=(k == 0), stop=(k == KC - 1))
                            return ins
                        op("pe", mm, reads=bw + [B("scol")], writes=[bp])
                        op("dve", lambda h, pp=pp, i=i, cb=cb: h.tensor_tensor(out=modc[i][:, cb * 4:(cb + 1) * 4, :], in0=pp[:], in1=adab[:, i, cb * 4:(cb + 1) * 4].unsqueeze(2).to_broadcast([128, 4, 2]), op=ALU.add), reads=[bp, B("adab")], writes=[B("modc")])
                    else:
                        c0 = cb * 512 - 2 * D
                        for cond in range(2):
                            pp, bp = gps[gi % 2], B("adagps%d" % (gi % 2))
                            gs, bg = gst[gi % 2], B("gst%d" % (gi % 2))
                            gi += 1

                            def mm(h, wb=wb, pp=pp, cond=cond):
                                ins = None
                                for k in range(KC):
                                    ins = h.matmul(pp[:], lhsT=scb[:, k, cond, :], rhs=wb[:, k, :], start=(k == 0), stop=(k == KC - 1))
                                return ins
                            op("pe", mm, reads=bw + [B("scb")], writes=[bp])
                            op("dve", lambda h, pp=pp, gs=gs, i=i, c0=c0: h.tensor_tensor(out=gs[:], in0=pp[:], in1=abr[:, i, c0:c0 + 512], op=ALU.add), reads=[bp, B("abr")], writes=[bg])
                            load(gb_d[i, cond, :, c0:c0 + 512], gs[:], reads=[bg])
                for i in range(2):
                    op("dve", lambda h, i=i: h.scalar_tensor_tensor(out=Acol[i][:], in0=modc[i][:, KC:2 * KC, :], scalar=1.0, in1=nwc[:, i, :].unsqueeze(2).to_broadcast([128, KC, 2]), op0=ALU.add, op1=ALU.mult), reads=[B("modc"), B("nwc")], writes=[B("Acol")])
                sch.barrier()

        def emit_norm(layer, b, hsrc, uT, bu, st, tts=None):
            cond = blk_cond(b)
            for tt in (range(NTT) if tts is None else tts):
                t0 = b * TB + tt * 128
                hb, bh = st["hb"][tt % 2], B("hb%d" % (tt % 2))
                load(hb[:], hsrc[t0:t0 + 128, :], writes=[bh])
                ssq, bq = st["ssq"], B("ssq")
                op("dve", lambda h: h.memset(ssq[:], 0.0), writes=[bq])
                op("act", lambda h, hb=hb: h.activation(out=st["junk"][:], in_=hb[:], func=AF.Square, accum_out=ssq[:, 0:1]), reads=[bh], writes=[B("junk"), bq])
                op("dve", lambda h: h.tensor_scalar(out=ssq[:, 1:2], in0=ssq[:, 0:1], scalar1=1.0 / D, scalar2=EPS, op0=ALU.mult, op1=ALU.add), writes=[bq])
                op("act", lambda h: h.sqrt(out=ssq[:, 2:3], in_=ssq[:, 1:2]), writes=[bq])
                op("dve", lambda h: h.reciprocal(out=ssq[:, 3:4], in_=ssq[:, 2:3]), writes=[bq])
                hn, bn = st["hn"][tt % 2], B("hn%d" % (tt % 2))
                op("dve", lambda h, hb=hb, hn=hn: h.tensor_scalar(out=hn[:], in0=hb[:], scalar1=ssq[:, 3:4], scalar2=None, op0=ALU.mult), reads=[bh, bq], writes=[bn])
                for k0 in range(0, KC, 4):
                    pt, bp = st["tps"][(k0 // 4) % 2], B("tps%d" % ((k0 // 4) % 2))

                    def tr(h, pt=pt, hn=hn, k0=k0):
                        ins = None
                        for kk in range(4):
                            ins = h.transpose(pt[:, kk, :], hn[:, (k0 + kk) * 128:(k0 + kk + 1) * 128], identb[:])
                        return ins
                    op("pe", tr, reads=[bn, bC], writes=[bp])
                    for kk in range(4):
                        k = k0 + kk
                        if kk % 2 == 0:
                            op("act", lambda h, pt=pt, kk=kk, k=k: h.activation(out=uT[:, k, tt * 128:(tt + 1) * 128], in_=pt[:, kk, :], func=AF.Identity, scale=Acol[layer][:, k, cond:cond + 1], bias=modc[layer][:, k, cond:cond + 1]), reads=[bp], writes=[bu])
                        else:
                            op("dve", lambda h, pt=pt, kk=kk, k=k: h.tensor_scalar(out=uT[:, k, tt * 128:(tt + 1) * 128], in0=pt[:, kk, :], scalar1=Acol[layer][:, k, cond:cond + 1], scalar2=modc[layer][:, k, cond:cond + 1], op0=ALU.mult, op1=ALU.add), reads=[bp], writes=[bu])

        def phase_norm_inproj(layer):
            with ExitStack() as e2:
                pst = [ps("gps%d" % i, [128, 512], F32, e2) for i in range(6)]
                st = {"hb": [sb("hb%d" % i, [128, D], F32, e2) for i in range(2)],
                      "hn": [sb("hn%d" % i, [128, D], BF16, e2) for i in range(2)],
                      "junk": sb("junk", [128, D], BF16, e2),
                      "ssq": sb("ssq", [128, 4], F32, e2),
                      "tps": [ps("tps%d" % i, [128, 4, 128], BF16, e2) for i in range(2)]}
                wbuf = [sb("inw%d" % i, [128, KC, 512], BF16, e2) for i in range(2)]
                wpb = [[B("inw%d_%d" % (i, k)) for k in range(KC // 8 + 1)] for i in range(2)]
                uTs = [sb("uT%d" % i, [128, KC, TB], BF16, e2) for i in range(2)]
                stg = [sb("stg%d" % i, [128, 512], BF16, e2) for i in range(3)]
                stgf = [sb("stgf%d" % i, [128, 512], F32, e2) for i in range(2)]
                cnt = {"p": 0, "s": 0, "f": 0}
                if layer == 1:
                    dtbb = sb("dtbb", [128, 2 * H], F32, e2)
                    spt = [sb("spt%d" % i, [128, 2 * H], F32, e2) for i in range(4)]
                e2.enter_context(nc.Block())
                if layer == 1:
                    load(dtbb[:], dt_bias.partition_broadcast(128), writes=[B("dtbb")])
                hsrc = x_in if layer == 0 else h1
                W = pin_w if layer == 0 else sin_w
                ncols = 2 * E if layer == 0 else SSD_IN
                cbs = [(c0, min(512, ncols - c0)) for c0 in range(0, ncols, 512)]

                def evac(b, c0, w, idx, pt, bp):
                    tsl = slice(b * TB + idx * 128, b * TB + (idx + 1) * 128)
                    if layer == 0 and c0 < E:
                        s, bs = stg[cnt["s"] % 3], B("stg%d" % (cnt["s"] % 3))
                        cnt["s"] += 1
                        op("act", lambda h: h.copy(out=s[:, :w], in_=pt[:, :w]), reads=[bp], writes=[bs])
                        load(xp_tok[tsl, c0:c0 + w], s[:, :w], reads=[bs])
                    elif layer == 0:
                        s, bs = stg[cnt["s"] % 3], B("stg%d" % (cnt["s"] % 3))
                        cnt["s"] += 1
                        op("act", lambda h: h.activation(out=s[:], in_=pt[:], func=AF.Silu), reads=[bp], writes=[bs])
                        r0 = c0 - E + idx * 128
                        load(zT[r0:r0 + 128, b * TB:(b + 1) * TB], s[:], reads=[bs])
                    elif c0 < DI:
                        s, bs = stg[cnt["s"] % 3], B("stg%d" % (cnt["s"] % 3))
                        cnt["s"] += 1
                        op("act", lambda h: h.activation(out=s[:, :w], in_=pt[:, :w], func=AF.Silu), reads=[bp], writes=[bs])
                        load(z_tok[tsl, c0:c0 + w], s[:, :w], reads=[bs])
                    elif c0 < DI + CONVCH:
                        s, bs = stg[cnt["s"] % 3], B("stg%d" % (cnt["s"] % 3))
                        cnt["s"] += 1
                        op("dve", lambda h: h.tensor_copy(out=s[:], in_=pt[:]), reads=[bp], writes=[bs])
                        r0 = c0 - DI + idx * 128
                        load(xbc_pre[r0:r0 + 128, b * TB:(b + 1) * TB], s[:], reads=[bs])
                    else:
                        t, bt = spt, B("spt")
                        W2 = 2 * H
                        op("dve", lambda h: h.tensor_tensor(out=t[0][:], in0=pt[:, :W2], in1=dtbb[:], op=ALU.add), reads=[bp, B("dtbb")], writes=[bt])
                        op("dve", lambda h: h.tensor_scalar(out=t[1][:], in0=t[0][:], scalar1=-1.0, scalar2=None, op0=ALU.mult), writes=[bt])
                        op("dve", lambda h: h.tensor_tensor(out=t[1][:], in0=t[1][:], in1=t[0][:], op=ALU.max), writes=[bt])
                        op("act", lambda h: h.activation(out=t[2][:], in_=t[1][:], func=AF.Exp, scale=-1.0), writes=[bt])
                        op("act", lambda h: h.activation(out=t[2][:], in_=t[2][:], func=AF.Ln, bias=1.0), writes=[bt])
                        op("dve", lambda h: h.tensor_scalar_max(out=t[1][:], in0=t[0][:], scalar1=0.0), writes=[bt])
                        op("dve", lambda h: h.tensor_tensor(out=t[3][:], in0=t[1][:], in1=t[2][:], op=ALU.add), writes=[bt])
                        load(dt_tok[tsl, :], t[3][:], reads=[bt])

                def is_tok(c0):
                    if layer == 0:
                        return c0 < E
                    return c0 < DI or c0 >= DI + CONVCH
                gidx = [(b, i) for b in range(NB) for i in range(len(cbs))]

                def get_w(n):
                    b_, i_ = gidx[n]
                    c0_, w_ = cbs[i_]
                    wb_, bw_ = wbuf[n % 2], wpb[n % 2]
                    if b_ == 0 or not USE_WCACHE:
                        wload(wb_, bw_, W, 0, KC, c0_, w_)
                        for j, k0 in enumerate(range(0, KC, 8)):
                            k1 = min(KC, k0 + 8)
                            if USE_WCACHE:
                                load(wcache[i_, :, k0:k1, :w_], wb_[:, k0:k1, :w_], reads=[bw_[j]], writes=[B("wc%d_%d" % (i_, j))])
                    else:
                        for j, k0 in enumerate(range(0, KC, 8)):
                            k1 = min(KC, k0 + 8)
                            load(wb_[:, k0:k1, :w_], wcache[i_, :, k0:k1, :w_], reads=[B("wc%d_%d" % (i_, j))], writes=[bw_[j]])
                get_w(0)
                emit_norm(layer, 0, hsrc, uTs[0], B("uT0"), st)
                for n, (b, i) in enumerate(gidx):
                    uT, bu = uTs[b % 2], B("uT%d" % (b % 2))
                    if b + 1 < NB and 1 <= i <= NTT:
                        emit_norm(layer, b + 1, hsrc, uTs[(b + 1) % 2], B("uT%d" % ((b + 1) % 2)), st, tts=[i - 1])
                    if n + 1 < len(gidx):
                        get_w(n + 1)
                    c0, w = cbs[i]
                    wb, bw = wbuf[n % 2], wpb[n % 2]
                    if is_tok(c0):
                        for tt in range(NTT):
                            pi = cnt["p"] % 6
                            cnt["p"] += 1
                            pt, bp = pst[pi], B("gps%d" % pi)

                            def mm(h, pt=pt, tt=tt, wb=wb, w=w):
                                ins = None
                                for k in range(KC):
                                    ins = h.matmul(pt[:, :w], lhsT=uT[:, k, tt * 128:(tt + 1) * 128], rhs=wb[:, k, :w], start=(k == 0), stop=(k == KC - 1))
                                return ins
                            op("pe", mm, reads=bw + [bu], writes=[bp])
                            evac(b, c0, w, tt, pt, bp)
                    else:
                        for ct in range(w // 128):
                            pi = cnt["p"] % 6
                            cnt["p"] += 1
                            pt, bp = pst[pi], B("gps%d" % pi)

                            def mm(h, pt=pt, ct=ct, wb=wb):
                                ins = None
                                for k in range(KC):
                                    ins = h.matmul(pt[:], lhsT=wb[:, k, ct * 128:(ct + 1) * 128], rhs=uT[:, k, :], start=(k == 0), stop=(k == KC - 1))
                                return ins
                            op("pe", mm, reads=bw + [bu], writes=[bp])
                            evac(b, c0, w, ct, pt, bp)
                sch.barrier()

        def phase_mid0():
            NG = PGW // 128
            rows = c.DSEQ // 64
            nts = c.DSEQ // 128
            with ExitStack() as e3:
                pps = [ps("pps%d" % i, [128, TB], F32, e3) for i in range(2)]
                gps_ = [ps("ggps%d" % i, [128, TB], F32, e3) for i in range(2)]
                cbps = ps("cbps", [128, 128], F32, e3)
                cps = ps("cps", [128, 2], F32, e3)
                colps = ps("colps", [128, 128], F32, e3)
                load_cols = mk_load_cols(e3, colps)
                psc = sb("psc", [128, E // 128], F32, e3)
                pgb = sb("pgb", [128, E // 128], F32, e3)
                pdl = [(w, d) for w in WINS for d in (-1, 0, 1)]
                PBt = sb("PBt", [128, len(pdl), 128], BF16, e3)
                Ft = sb("Ft", [128, 4, 128], BF16, e3)
                sdl = []
                for w in WINS:
                    for d in range(-5, 6):
                        if any(-w // 2 <= 2 * d + a - bb <= w // 2 - 1 for a in (0, 1) for bb in (0, 1)):
                            sdl.append((w, d))
                SBt = sb("SBt", [128, len(sdl), 128], BF16, e3)
                ncol = sb("ncol", [128, 1], F32, e3)
                Dt = {"p": sb("Dblk_p", [128, 4, 2, 128], BF16, e3), "s": sb("Dblk_s", [128, 4, nts, 128], BF16, e3)}
                It = {"p": sb("icnt_p", [128, 4, 2 * 128], F32, e3), "s": sb("icnt_s", [128, 4, nts * 128], F32, e3)}
                MAXI = 12
                xpl = [sb("xpl%d" % i, [128, MAXI, 512], BF16, e3) for i in range(2)]
                mixT = [sb("mixT%d" % i, [128, NG, TB], BF16, e3) for i in range(2)]
                gw = [sb("gw%d" % i, [128, NG, 512], BF16, e3) for i in range(2)]
                gwb = [[B("gw%d_%d" % (i, k)) for k in range(NG // 8 + 1)] for i in range(2)]
                szt = [sb("szt%d" % i, [128, TB], BF16, e3) for i in range(4)]
                t1 = [sb("t1_%d" % i, [128, TB], F32, e3) for i in range(2)]
                gto = [sb("gto%d" % i, [128, TB], BF16, e3) for i in range(2)]
                e3.enter_context(nc.Block())
                load_cols(pscale, lambda i0, m: psc[:, i0:i0 + m], E // 128, B("psc"))
                load_cols(pgrp_b, lambda i0, m: pgb[:, i0:i0 + m], E // 128, B("pgb"))
                op("dve", lambda h: h.tensor_tensor(out=pgb[:], in0=pgb[:], in1=psc[:], op=ALU.mult), reads=[B("psc")], writes=[B("pgb")])
                bK = B("poolconst")
                PBk = {}
                SBk = {}

                def band(h, ap, lo, hi, off):
                    h.memset(ap, 1.0)
                    h.affine_select(out=ap, in_=ap, pattern=[[-1, 128]], compare_op=ALU.is_ge, fill=0.0, base=off - lo, channel_multiplier=1)
                    return h.affine_select(out=ap, in_=ap, pattern=[[1, 128]], compare_op=ALU.is_ge, fill=0.0, base=hi - off, channel_multiplier=-1)

                def mkbands(h):
                    ins = None
                    for n, (w, d) in enumerate(pdl):
                        ins = band(h, PBt[:, n, :], -w // 2, w // 2 - 1, 128 * d)
                        PBk[(w, d)] = PBt[:, n, :]
                    for n, w in enumerate(WINS):
                        ins = band(h, Ft[:, n, :], -w // 2, w // 2 - 1, 0)
                    ins = h.memset(SBt[:], 0.0)
                    return ins
                op("pool", mkbands, writes=[bK])

                def mksb(h):
                    ins = None
                    for n, (w, d) in enumerate(sdl):
                        SBk[(w, d)] = SBt[:, n, :]
                        wi = WINS.index(w)
                        for a in (0, 1):
                            for bb in (0, 1):
                                if -w // 2 <= 2 * d + a - bb <= w // 2 - 1:
                                    ins = h.tensor_copy(out=SBt[64 * a:64 * a + 64, n, 64 * bb:64 * bb + 64], in_=Ft[64 * a:64 * a + 64, wi, 64 * a:64 * a + 64])
                    return ins
                op("dve", mksb, reads=[bK], writes=[bK])
                def pblk(w, i, j):
                    return PBk[(w, i - j)] if 0 <= i < 2 else None

                def sblk(w, i, j):
                    return SBk.get((w, i - j)) if 0 <= i < nts else None
                plans = {"p": (2, pblk), "s": (nts, sblk)}
                Dblk = {}
                icnt = {}
                for typ in ("p", "s"):
                    nt, bf = plans[typ]
                    dt_ = Dt[typ]
                    ic_ = It[typ]
                    for wi, w in enumerate(WINS):
                        for j in range(nt):
                            il = [i for i in range(j - 6, j + 7) if bf(w, i, j) is not None]

                            def cm(h, il=il, w=w, j=j, bf=bf):
                                for n, i in enumerate(il):
                                    h.matmul(cps[:, 0:1], lhsT=bf(w, i, j), rhs=onesb[:, 0:1], start=(n == 0), stop=(n == len(il) - 1))
                                ins = None
                                for n, i in enumerate(il):
                                    ins = h.matmul(cbps[:], lhsT=onesb[:], rhs=bf(w, i, j), start=(n == 0), stop=(n == len(il) - 1))
                                return ins
                            op("pe", cm, reads=[bK, bC], writes=[B("cps")])
                            op("dve", lambda h: h.tensor_scalar(out=ncol[:], in0=cps[:, 0:1], scalar1=-1.0, scalar2=None, op0=ALU.mult), reads=[B("cps")], writes=[B("ncol")])
                            op("dve", lambda h, dt_=dt_, wi=wi, j=j, w=w, bf=bf: h.scalar_tensor_tensor(out=dt_[:, wi, j, :], in0=identb[:], scalar=ncol[:, 0:1], in1=bf(w, j, j), op0=ALU.mult, op1=ALU.add), reads=[B("ncol"), bK], writes=[B("dblk")])
                            op("dve", lambda h, ic_=ic_, wi=wi, j=j: h.reciprocal(out=ic_[:, wi, j * 128:(j + 1) * 128], in_=cbps[:]), reads=[B("cps")], writes=[B("icnt"), B("cps")])
                            Dblk[(typ, w, j)] = dt_[:, wi, j, :]
                    icnt[typ] = ic_
                cn = {"x": 0, "p": 0, "m": 0, "w": 0, "g": 0}

                def blk_outs(b):
                    typ = "p" if b * TB < TP else "s"
                    outs = []
                    for jj in range(NTT):
                        at = b * NTT + jj
                        if typ == "p":
                            outs.append((at, (at // 2) * 2, at % 2))
                        else:
                            outs.append((at, TP // 128, at - TP // 128))
                    return typ, outs
                xitems = []
                szitems = []
                for b in range(NB):
                    typ, outs = blk_outs(b)
                    bf = plans[typ][1]
                    for gi, w in enumerate(WINS):
                        need = sorted({base + i for (at, base, j) in outs for i in range(j - 6, j + 7) if bf(w, i, j) is not None})
                        for sub in range(0, PGW, 512):
                            xitems.append((need, gi * PGW + sub, min(512, PGW - sub)))
                        for cb in range(0, PGW, 512):
                            for ct in range(min(512, PGW - cb) // 128):
                                szitems.append((b, gi * PGW + cb + ct * 128))

                def issue_x(m):
                    need, c0, sw = xitems[m]
                    assert need == list(range(need[0], need[-1] + 1)) and len(need) <= MAXI
                    load(xpl[m % 2][:, 0:len(need), :sw], xp_tok[need[0] * 128:(need[-1] + 1) * 128, c0:c0 + sw].rearrange("(t p) c -> p t c", p=128), writes=[B("xpl%d" % (m % 2))])

                def issue_sz(q):
                    b_, ech_ = szitems[q]
                    load(szt[q % 4][:], zT[ech_:ech_ + 128, b_ * TB:(b_ + 1) * TB], writes=[B("szt%d" % (q % 4))])
                issue_x(0)
                issue_sz(0)
                issue_sz(1)
                for b in range(NB):
                    typ = "p" if b * TB < TP else "s"
                    nt, bf = plans[typ]
                    outs = []
                    for jj in range(NTT):
                        at = b * NTT + jj
                        if typ == "p":
                            outs.append((at, (at // 2) * 2, at % 2))
                        else:
                            outs.append((at, TP // 128, at - TP // 128))
                    for gi, w in enumerate(WINS):
                        need = sorted({base + i for (at, base, j) in outs for i in range(j - 6, j + 7) if bf(w, i, j) is not None})
                        assert len(need) <= MAXI
                        lidx = {a: n for n, a in enumerate(need)}
                        mx, bm = mixT[cn["m"] % 2], B("mixT%d" % (cn["m"] % 2))
                        cn["m"] += 1
                        for sub in range(0, PGW, 512):
                            sw = min(512, PGW - sub)
                            xl, bx = xpl[cn["x"] % 2], B("xpl%d" % (cn["x"] % 2))
                            assert xitems[cn["x"]][1] == gi * PGW + sub
                            if cn["x"] + 1 < len(xitems):
                                issue_x(cn["x"] + 1)
                            cn["x"] += 1
                            for ct in range(sw // 128):
                                pp, bp = pps[cn["p"] % 2], B("pps%d" % (cn["p"] % 2))
                                cn["p"] += 1

                                def pm(h, pp=pp, xl=xl, ct=ct, w=w, outs=outs, bf=bf, lidx=lidx, typ=typ):
                                    ins = None
                                    for jj, (at, base, j) in enumerate(outs):
                                        il = [i for i in range(j - 6, j + 7) if bf(w, i, j) is not None]
                                        for n, i in enumerate(il):
                                            blk = Dblk[(typ, w, j)] if i == j else bf(w, i, j)
                                            ins = h.matmul(pp[:, jj * 128:(jj + 1) * 128], lhsT=xl[:, lidx[base + i], ct * 128:(ct + 1) * 128], rhs=blk, start=(n == 0), stop=(n == len(il) - 1))
                                    return ins
                                op("pe", pm, reads=[bx, bK, B("dblk")], writes=[bp])
                                cti = sub // 128 + ct
                                if typ == "p":
                                    ic = icnt["p"][:, gi, :].unsqueeze(1).to_broadcast([128, TB // 256, 256])
                                    op("dve", lambda h, pp=pp, mx=mx, cti=cti, ic=ic: h.tensor_tensor(out=mx[:, cti, :].rearrange("p (s t) -> p s t", t=256), in0=pp[:].rearrange("p (s t) -> p s t", t=256), in1=ic, op=ALU.mult), reads=[bp, B("icnt")], writes=[bm])
                                else:
                                    j0 = outs[0][2]
                                    op("dve", lambda h, pp=pp, mx=mx, cti=cti, j0=j0, gi=gi: h.tensor_tensor(out=mx[:, cti, :], in0=pp[:], in1=icnt["s"][:, gi, j0 * 128:j0 * 128 + TB], op=ALU.mult), reads=[bp, B("icnt")], writes=[bm])
                        for cb in range(0, PGW, 512):
                            cw = min(512, PGW - cb)
                            wb, bw = gw[cn["w"] % 2], gwb[cn["w"] % 2]
                            cn["w"] += 1
                            wload(wb, bw, pgrp_w[gi], 0, NG, cb, cw)
                            for ct in range(cw // 128):
                                ech = gi * PGW + cb + ct * 128
                                i2 = cn["g"] % 2
                                i4 = cn["g"] % 4
                                assert szitems[cn["g"]] == (b, ech)
                                if cn["g"] + 2 < len(szitems):
                                    issue_sz(cn["g"] + 2)
                                cn["g"] += 1
                                pp, bp = gps_[i2], B("ggps%d" % i2)

                                def gm(h, pp=pp, wb=wb, ct=ct, mx=mx):
                                    ins = None
                                    for k in range(NG):
                                        ins = h.matmul(pp[:], lhsT=wb[:, k, ct * 128:(ct + 1) * 128], rhs=mx[:, k, :], start=(k == 0), stop=(k == NG - 1))
                                    return ins
                                op("pe", gm, reads=bw + [bm], writes=[bp])
                                ec = ech // 128
                                op("act", lambda h, pp=pp, i2=i2, ec=ec: h.activation(out=t1[i2][:], in_=pp[:], func=AF.Identity, scale=psc[:, ec:ec + 1], bias=pgb[:, ec:ec + 1]), reads=[bp, B("pgb")], writes=[B("t1_%d" % i2)])
                                op("dve", lambda h, i2=i2, i4=i4: h.tensor_tensor(out=gto[i2][:], in0=t1[i2][:], in1=szt[i4][:], op=ALU.mult), reads=[B("t1_%d" % i2), B("szt%d" % i4)], writes=[B("gto%d" % i2)])
                                load(gT[ech:ech + 128, b * TB:(b + 1) * TB], gto[i2][:], reads=[B("gto%d" % i2)])
                sch.barrier()

        def phase_outproj(layer):
            Wm = pout_w if layer == 0 else sout_w
            hsrc = x_in if layer == 0 else h1
            hdst = h1 if layer == 0 else h2
            KE = E // 128
            KQ = 16
            with ExitStack() as e4:
                pst = [ps("ops%d" % i, [128, 512], F32, e4) for i in range(8)]
                gTb, bg = sb("gTb", [128, KE, TB], BF16, e4), B("gTb")
                Gb = sb("Gb", [128, 2, D], F32, e4)
                wq = [sb("wq%d" % i, [128, KQ, 512], BF16, e4) for i in range(4)]
                wqb = [[B("wq%d_%d" % (i, k)) for k in range(KQ // 8 + 1)] for i in range(4)]
                ht = [sb("ht%d" % i, [128, 512], F32, e4) for i in range(8)]
                tt_ = [sb("ot%d" % i, [128, 512], F32, e4) for i in range(3)]
                e4.enter_context(nc.Block())
                for cond in range(2):
                    load(Gb[:, cond, :], gb_d[layer, cond], writes=[B("Gb")])
                pieces = [(b, cb, kq) for b in range(NB) for cb in range(D // 512) for kq in range(KE // KQ)]

                def issue(n):
                    b, cb, kq = pieces[n]
                    pid = cb * (KE // KQ) + kq
                    wb_, bw_ = wq[n % 4], wqb[n % 4]
                    if b == 0:
                        wload(wb_, bw_, Wm, kq * KQ * 128, KQ, cb * 512, 512)
                        for j, k0 in enumerate(range(0, KQ, 8)):
                            load(wcache2[pid, :, k0:k0 + 8, :], wb_[:, k0:k0 + 8, :], reads=[bw_[j]], writes=[B("wc2_%d_%d" % (pid, j))])
                    else:
                        for j, k0 in enumerate(range(0, KQ, 8)):
                            load(wb_[:, k0:k0 + 8, :], wcache2[pid, :, k0:k0 + 8, :], reads=[B("wc2_%d_%d" % (pid, j))], writes=[bw_[j]])
                for n in range(min(3, len(pieces))):
                    issue(n)
                pcnt = 0
                hc = 0
                for n, (b, cb, kq) in enumerate(pieces):
                    cond = blk_cond(b)
                    if cb == 0 and kq == 0:
                        for k0 in range(0, KE, 16):
                            load(gTb[:, k0:k0 + 16, :], gT[k0 * 128:(k0 + 16) * 128, b * TB:(b + 1) * TB].rearrange("(k p) t -> p k t", p=128), writes=[bg] if k0 == 0 else [B("gTb_%d" % k0)])
                    if n + 3 < len(pieces):
                        issue(n + 3)
                    if kq == 0:
                        base = pcnt
                        pcnt += NTT
                        hbase = hc
                        for tt in range(NTT):
                            load(ht[(hbase + tt) % 8][:], hsrc[b * TB + tt * 128:b * TB + (tt + 1) * 128, cb * 512:(cb + 1) * 512], writes=[B("ht%d" % ((hbase + tt) % 8))])
                        hc += NTT
                    wb, bw = wq[n % 4], wqb[n % 4]
                    for tt in range(NTT):
                        pi = (base + tt) % 8
                        pt, bp = pst[pi], B("ops%d" % pi)

                        def mm(h, pt=pt, tt=tt, wb=wb, kq=kq):
                            ins = None
                            for k in range(KQ):
                                kk = kq * KQ + k
                                ins = h.matmul(pt[:], lhsT=gTb[:, kk, tt * 128:(tt + 1) * 128], rhs=wb[:, k, :], start=(kk == 0), stop=(kk == KE - 1))
                            return ins
                        op("pe", mm, reads=bw + [bg] + [B("gTb_%d" % k0) for k0 in range(16, KE, 16)], writes=[bp] if kq == 0 else [], )
                        if kq != 0:
                            bp.w = ("c", "pe", sch.S["pe"].cnt)
                        if kq == KE // KQ - 1:
                            tsl = slice(b * TB + tt * 128, b * TB + (tt + 1) * 128)
                            csl = slice(cb * 512, (cb + 1) * 512)
                            i3 = (hbase + tt) % 3
                            i8 = (hbase + tt) % 8
                            if layer == 0:
                                op("dve", lambda h, pt=pt, i3=i3, cond=cond, csl=csl: h.tensor_tensor(out=tt_[i3][:], in0=pt[:], in1=Gb[:, cond, csl], op=ALU.mult), reads=[bp, B("Gb")], writes=[B("ot%d" % i3)])
                            else:
                                ti = (b * TB + tt * 128) // 128
                                op("dve", lambda h, pt=pt, i3=i3, cond=cond, csl=csl, ti=ti: h.scalar_tensor_tensor(out=tt_[i3][:], in0=pt[:], scalar=rstdy[:, ti:ti + 1], in1=Gb[:, cond, csl], op0=ALU.mult, op1=ALU.mult), reads=[bp, B("Gb")], writes=[B("ot%d" % i3)])
                            op("pool", lambda h, i3=i3, i8=i8: h.tensor_tensor(out=tt_[i3][:], in0=tt_[i3][:], in1=ht[i8][:], op=ALU.add), reads=[B("ht%d" % i8)], writes=[B("ot%d" % i3)])
                            load(hdst[tsl, csl], tt_[i3][:], reads=[B("ot%d" % i3)])
                sch.barrier()

        seqs = [(i * c.SEQ, c.SEQ, "p", i) for i in range(c.NPS)] + [(TP, c.DSEQ, "s", 0)]

        def phase_conv():
            NCH = CONVCH // 128
            LM = max(c.SEQ, c.DSEQ)
            with ExitStack() as e5:
                cvps = [ps("cvps%d" % i, [128, 512], F32, e5) for i in range(4)]
                tps = [ps("ctps%d" % i, [128, 4, 128], BF16, e5) for i in range(2)]
                colps = ps("colps", [128, 128], F32, e5)
                load_cols = mk_load_cols(e5, colps)
                cwc = sb("cwc", [128, 7, NCH], F32, e5)
                cbc = sb("cbc", [128, NCH], F32, e5)
                pre = [sb("pre%d" % i, [128, LM + 6], BF16, e5) for i in range(3)]
                dg = [sb("dg%d" % i, [128, 7, 128], BF16, e5) for i in range(2)]
                post = [sb("post%d" % i, [128, LM], BF16, e5) for i in range(2)]
                tst = [sb("tst%d" % i, [128, LM // 128, 128], BF16, e5) for i in range(2)]
                e5.enter_context(nc.Block())
                for k in range(7):
                    load_cols(conv_w[k], lambda i0, m, k=k: cwc[:, k, i0:i0 + m], NCH, B("cwc"))
                load_cols(conv_b, lambda i0, m: cbc[:, i0:i0 + m], NCH, B("cwc"))
                for i in range(3):
                    op("pool", lambda h, i=i: h.memset(pre[i][:], 0.0), writes=[B("pre%d" % i)])
                tc = 0
                pc = 0
                items = [(s0, L, ct) for (s0, L, typ, si) in seqs for ct in range(NCH)]

                def issue_pre(n):
                    s0, L, ct = items[n]
                    load(pre[n % 3][:, 3:3 + L], xbc_pre[ct * 128:(ct + 1) * 128, s0:s0 + L], writes=[B("pre%d" % (n % 3))])
                issue_pre(0)
                issue_pre(1)

                def mk_dg(n):
                    ct_ = items[n][2]
                    op("dve", lambda h: h.tensor_tensor(out=dg[n % 2][:], in0=identb[:].unsqueeze(1).to_broadcast([128, 7, 128]), in1=cwc[:, :, ct_:ct_ + 1].to_broadcast([128, 7, 128]), op=ALU.mult), reads=[B("cwc"), bC], writes=[B("dg%d" % (n % 2))])
                for n, (s0, L, ct) in enumerate(items):
                    if True:
                        if n + 2 < len(items):
                            issue_pre(n + 2)
                        i2 = n % 2
                        i3 = n % 3
                        pr, po, dgi = pre[i3], post[i2], dg[i2]
                        bpr, bpo, bdg = B("pre%d" % i3), B("post%d" % i2), B("dg%d" % i2)
                        if n == 0:
                            mk_dg(0)
                        if n + 1 < len(items):
                            mk_dg(n + 1)
                        for q0 in range(0, L, 512):
                            qw = min(512, L - q0)
                            cp, bcp = cvps[pc % 4], B("cvps%d" % (pc % 4))
                            pc += 1

                            def cm(h, cp=cp, pr=pr, dgi=dgi, q0=q0, qw=qw):
                                ins = None
                                for k in range(7):
                                    ins = h.matmul(cp[:, :qw], lhsT=dgi[:, k, :], rhs=pr[:, q0 + k:q0 + k + qw], start=(k == 0), stop=(k == 6))
                                return ins
                            op("pe", cm, reads=[bpr, bdg], writes=[bcp])
                            op("act", lambda h, cp=cp, po=po, q0=q0, qw=qw, ct=ct: h.activation(out=po[:, q0:q0 + qw], in_=cp[:, :qw], func=AF.Silu, bias=cbc[:, ct:ct + 1]), reads=[bcp, B("cwc")], writes=[bpo])
                        load(xbcT[ct * 128:(ct + 1) * 128, s0:s0 + L], po[:, :L], reads=[bpo])
                        if ct < (DI + GN) // 128:
                            ts_, bts = tst[i2], B("tst%d" % i2)
                            for q0 in range(0, L // 128, 4):
                                pt, bp = tps[tc % 2], B("ctps%d" % (tc % 2))
                                tc += 1
                                nq = min(4, L // 128 - q0)

                                def tr(h, pt=pt, po=po, q0=q0, nq=nq):
                                    ins = None
                                    for q in range(nq):
                                        ins = h.transpose(pt[:, q, :], po[:, (q0 + q) * 128:(q0 + q + 1) * 128], identb[:])
                                    return ins
                                op("pe", tr, reads=[bpo, bC], writes=[bp])
                                op("dve", lambda h, pt=pt, ts_=ts_, q0=q0, nq=nq: h.tensor_copy(out=ts_[:, q0:q0 + nq, :], in_=pt[:, :nq, :]), reads=[bp], writes=[bts])
                            load(xb_tok[s0:s0 + L, ct * 128:(ct + 1) * 128].rearrange("(t p) c -> p t c", p=128), ts_[:, :L // 128, :], reads=[bts])
                sch.barrier()

        def phase_scan():
            NU = H // 4
            NCD = DI // 128
            with ExitStack() as e6:
                dps = [ps("dps%d" % i, [128, 512], F32, e6) for i in range(2)]
                yps = [ps("yps%d" % i, [128, 2, 256], F32, e6) for i in range(2)]
                scps = ps("scps", [128, G // 2, 128], F32, e6)
                aps_ = ps("aps", [128, 2, H], F32, e6)
                tps = [ps("stps%d" % i, [128, 4, 128], BF16, e6) for i in range(1)]
                fps = ps("fps", [128, 128], F32, e6)
                load_cols = mk_load_cols(e6, fps)
                gnc = sb("gnc", [128, NCD], F32, e6)
                Ab = sb("Ab", [128, 2 * H], F32, e6)
                Db = sb("Db", [128, H], F32, e6)
                xt = [sb("xt%d" % i, [128, DI], BF16, e6) for i in range(1)] * 2
                xd = sb("xd", [128, DI], BF16, e6)
                bt_ = [sb("bt%d" % i, [128, GN], BF16, e6) for i in range(2)]
                BTt = [sb("BT%d" % i, [128, G, 128], BF16, e6) for i in range(2)]
                CTt = [sb("CT%d" % i, [128, G, 128], BF16, e6) for i in range(2)]
                dtt = [sb("dtt%d" % i, [128, H], F32, e6) for i in range(2)]
                zt = sb("zt", [128, DI], BF16, e6)
                HT = sb("HT", [128, DI], F32, e6)
                HTb = sb("HTb", [128, DI], BF16, e6)
                xdt = sb("xdt", [128, DI], BF16, e6)
                xdts = sb("xdts", [128, DI], BF16, e6)
                ybf = sb("ybf", [128, DI], BF16, e6)
                gstg = [sb("gstg%d" % i, [128, 16, 128], BF16, e6) for i in range(2)]
                ybch = sb("ybch", [128, DI], BF16, e6)
                sm = sb("sm", [128, 6, H], F32, e6)
                scm = sb("scm", [128, G, 128], BF16, e6)
                ssqp = sb("ssqp", [128, NU + 4], F32, e6)
                junk = sb("sjunk", [128, 256], F32, e6)
                rhsb = [sb("rhsb%d" % i, [128, 4, 128], F32, e6) for i in range(2)]
                Eb = [sb("Eb%d" % i, [128, 4, 128], BF16, e6) for i in range(2)]
                MT = [sb("MT%d" % i, [128, 4, 128], BF16, e6) for i in range(2)]
                yu = [sb("yu%d" % i, [128, 256], F32, e6) for i in range(3)]
                sts = sb("sts", [128, 128], F32, e6)
                e6.enter_context(nc.Block())
                load_cols(gn_w, lambda i0, m: gnc[:, i0:i0 + m], NCD, B("gnc"))
                load(Ab[:], a_log.partition_broadcast(128), writes=[B("Ab")])
                load(Db[:], d_skip.partition_broadcast(128), writes=[B("Db")])
                op("act", lambda h: h.activation(out=Ab[:], in_=Ab[:], func=AF.Exp), writes=[B("Ab")])
                op("dve", lambda h: h.tensor_scalar(out=Ab[:], in0=Ab[:], scalar1=-1.0, scalar2=None, op0=ALU.mult), writes=[B("Ab")])
                uc = [0]
                pw = min(512, HPG * 64)
                bHT = {c0: B("HT_%d" % c0) for c0 in range(0, DI, pw)}
                bHTb = {c0: B("HTb_%d" % c0) for c0 in range(0, DI, pw)}

                def chunk_loads(s0, cidx, d, i2):
                    t0 = s0 + cidx * 128
                    load(xt[0][:], xb_tok[t0:t0 + 128, 0:DI], writes=[B("xt0")])
                    load(bt_[i2][:], xb_tok[t0:t0 + 128, DI:DI + GN], writes=[B("bt%d" % i2)])
                    load(BTt[i2][:], xbcT[DI:DI + GN, t0:t0 + 128].rearrange("(g n) t -> n g t", n=128), writes=[B("BT%d" % i2)])
                    load(CTt[i2][:], xbcT[DI + GN:DI + 2 * GN, t0:t0 + 128].rearrange("(g n) t -> n g t", n=128), writes=[B("CT%d" % i2)])
                    load(dtt[i2][:], dt_tok[t0:t0 + 128, d * H:(d + 1) * H], writes=[B("dtt%d" % i2)])

                def chunk(s0, cidx, d, final, gchunk, i2, nxt):
                    t0 = s0 + cidx * 128
                    X, bX = xt[0], B("xt0")
                    Bt, bBt = bt_[i2], B("bt%d" % i2)
                    BT, bBT = BTt[i2], B("BT%d" % i2)
                    CT, bCT = CTt[i2], B("CT%d" % i2)
                    dtc, bdt = dtt[i2], B("dtt%d" % i2)
                    Ld, Ud = (Lf, Uf) if d == 0 else (Lb, Ub)
                    if final:
                        load(zt[:], z_tok[t0:t0 + 128, :], writes=[B("zt")])
                    bs = B("sm")
                    op("dve", lambda h: h.tensor_tensor(out=sm[:, 0, :], in0=dtc[:], in1=Ab[:, d * H:(d + 1) * H], op=ALU.mult), reads=[bdt, B("Ab")], writes=[bs])

                    def am(h):
                        h.matmul(aps_[:, 0, :], lhsT=Ld[:], rhs=sm[:, 0, :], start=True, stop=True)
                        return h.matmul(aps_[:, 1, :], lhsT=onesf[:], rhs=sm[:, 0, :], start=True, stop=True)
                    op("pe", am, reads=[bs, bC], writes=[B("aps")])
                    op("act", lambda h: h.activation(out=sm[:, 1, :], in_=aps_[:, 0, :], func=AF.Exp), reads=[B("aps")], writes=[bs])
                    op("dve", lambda h: h.tensor_copy(out=sm[:, 5, :], in_=aps_[:, 0, :]), reads=[B("aps")], writes=[bs])
                    op("dve", lambda h: h.tensor_tensor(out=sm[:, 2, :], in0=aps_[:, 1, :], in1=sm[:, 5, :], op=ALU.subtract), reads=[B("aps")], writes=[bs])
                    op("act", lambda h: h.activation(out=sm[:, 2, :], in_=sm[:, 2, :], func=AF.Exp), writes=[bs])
                    op("act", lambda h: h.activation(out=sm[:, 3, :], in_=aps_[:, 1, :], func=AF.Exp), reads=[B("aps")], writes=[bs])
                    op("dve", lambda h: h.tensor_tensor(out=sm[:, 4, :], in0=dtc[:], in1=sm[:, 2, :], op=ALU.mult), writes=[bs])
                    op("dve", lambda h: h.tensor_tensor(out=xdt[:].rearrange("p (h q) -> p h q", q=64), in0=X[:].rearrange("p (h q) -> p h q", q=64), in1=dtc[:].unsqueeze(2).to_broadcast([128, H, 64]), op=ALU.mult), reads=[bX, bdt], writes=[B("xdt")])
                    op("dve", lambda h: h.tensor_tensor(out=xdts[:].rearrange("p (h q) -> p h q", q=64), in0=X[:].rearrange("p (h q) -> p h q", q=64), in1=sm[:, 4, :].unsqueeze(2).to_broadcast([128, H, 64]), op=ALU.mult), reads=[bX, bs], writes=[B("xdts")])
                    for gh in range(2):
                        def smm(h, gh=gh):
                            ins = None
                            for gg in range(G // 2):
                                g_ = gh * (G // 2) + gg
                                ins = h.matmul(scps[:, gg, :], lhsT=BT[:, g_, :], rhs=CT[:, g_, :], start=True, stop=True)
                            return ins
                        op("pe", smm, reads=[bBT, bCT], writes=[B("scps")])
                        op("dve", lambda h, gh=gh: h.tensor_tensor(out=scm[:, gh * (G // 2):(gh + 1) * (G // 2), :], in0=scps[:], in1=Ld[:].unsqueeze(1).to_broadcast([128, G // 2, 128]), op=ALU.mult), reads=[B("scps"), bC], writes=[B("scm")])
                    if final:
                        op("dve", lambda h: h.memset(ssqp[:], 0.0), writes=[B("ssqp")])
                        load(ybch[:], ybd[t0:t0 + 128, :], reads=[B("ybd%d" % gchunk)], writes=[B("ybch")])
                        op("dve", lambda h: h.tensor_tensor(out=xd[:].rearrange("p (h q) -> p h q", q=64), in0=X[:].rearrange("p (h q) -> p h q", q=64), in1=Db[:].unsqueeze(2).to_broadcast([128, H, 64]), op=ALU.mult), reads=[bX, B("Db")], writes=[B("xd")])
                    ubase = uc[0]
                    if nxt is not None:
                        chunk_loads(nxt[0], nxt[1], nxt[2], 1 - i2)

                    def S1(u):
                        h0 = 4 * u
                        k2 = (ubase + u) % 2
                        op("dve", lambda h, k2=k2, h0=h0: h.tensor_tensor(out=rhsb[k2][:], in0=sm[:, 0, h0:h0 + 4].unsqueeze(2).to_broadcast([128, 4, 128]), in1=Ld[:].unsqueeze(1).to_broadcast([128, 4, 128]), op=ALU.mult), reads=[bs, bC], writes=[B("rhsb%d" % k2)])
                        op("pe", lambda h, k2=k2: h.matmul(dps[k2][:], lhsT=Ud[:], rhs=rhsb[k2][:].rearrange("p a b -> p (a b)"), start=True, stop=True), reads=[B("rhsb%d" % k2), bC], writes=[B("dps%d" % k2)])
                        op("act", lambda h, k2=k2: h.activation(out=Eb[k2][:].rearrange("p a b -> p (a b)"), in_=dps[k2][:], func=AF.Exp), reads=[B("dps%d" % k2)], writes=[B("Eb%d" % k2)])

                    def S2(u):
                        h0 = 4 * u
                        g_ = h0 // HPG
                        k2 = (ubase + u) % 2
                        k3 = u % 3
                        Y, bY = yu[k3], B("yu%d" % k3)
                        csl = slice(h0 * 64, h0 * 64 + 256)
                        op("dve", lambda h, k2=k2, g_=g_: h.tensor_tensor(out=MT[k2][:], in0=Eb[k2][:], in1=scm[:, g_, :].unsqueeze(1).to_broadcast([128, 4, 128]), op=ALU.mult), reads=[B("Eb%d" % k2), B("scm")], writes=[B("MT%d" % k2)])

                        def ym(h, k2=k2, h0=h0, g_=g_):
                            if final:
                                h.matmul(yps[k2][:, 0, :], lhsT=identb[:], rhs=ybch[:, csl], start=True, stop=False)
                                h.matmul(yps[k2][:, 0, :], lhsT=identb[:], rhs=xd[:, csl], start=False, stop=False, skip_group_check=True)
                            for hh in range(4):
                                h.matmul(yps[k2][:, 0, hh * 64:(hh + 1) * 64], lhsT=MT[k2][:, hh, :], rhs=xdt[:, (h0 + hh) * 64:(h0 + hh + 1) * 64], start=(not final), stop=True, skip_group_check=True)
                            return h.matmul(yps[k2][:, 1, :], lhsT=CT[:, g_, :], rhs=HTb[:, h0 * 64:h0 * 64 + 256], start=True, stop=True)
                        rds = [B("MT%d" % k2), B("xdt"), bCT, bHTb[(h0 * 64 // pw) * pw], bC] + ([B("ybch"), B("xd")] if final else [])
                        op("pe", ym, reads=rds, writes=[B("yps%d" % k2)])

                        def sc(h, k2=k2, Y=Y, h0=h0):
                            ins = None
                            for hh in range(4):
                                ins = h.activation(out=Y[:, hh * 64:(hh + 1) * 64], in_=yps[k2][:, 1, hh * 64:(hh + 1) * 64], func=AF.Identity, scale=sm[:, 1, h0 + hh:h0 + hh + 1])
                            return ins
                        op("act", sc, reads=[B("yps%d" % k2), bs], writes=[bY])

                    def S3(u):
                        h0 = 4 * u
                        k2 = (ubase + u) % 2
                        k3 = u % 3
                        Y, bY = yu[k3], B("yu%d" % k3)
                        csl = slice(h0 * 64, h0 * 64 + 256)
                        if not final:
                            op("dve", lambda h, k2=k2, Y=Y: h.tensor_tensor(out=ybch[:, csl], in0=Y[:], in1=yps[k2][:, 0, :], op=ALU.add), reads=[B("yps%d" % k2), bY], writes=[B("ybch")])
                        else:
                            op("dve", lambda h, k2=k2, Y=Y: h.tensor_tensor(out=Y[:], in0=Y[:], in1=yps[k2][:, 0, :], op=ALU.add), reads=[B("yps%d" % k2)], writes=[bY])

                    def S4(u):
                        h0 = 4 * u
                        k3 = u % 3
                        Y, bY = yu[k3], B("yu%d" % k3)
                        csl = slice(h0 * 64, h0 * 64 + 256)
                        op("dve", lambda h, Y=Y: h.tensor_tensor(out=ybf[:, csl], in0=Y[:], in1=zt[:, csl], op=ALU.mult), reads=[B("zt"), bY], writes=[B("ybf")])
                    for i in range(-2, NU + 1):
                        if 0 <= i + 2 < NU:
                            S1(i + 2)
                        if 0 <= i + 1 < NU:
                            S2(i + 1)
                        if 0 <= i < NU:
                            S3(i)
                        if final and 0 <= i - 1 < NU:
                            S4(i - 1)
                    uc[0] += NU
                    if not final:
                        load(ybd[t0:t0 + 128, :], ybch[:], reads=[B("ybch")], writes=[B("ybd%d" % gchunk)])
                    else:
                        op("act", lambda h: h.activation(out=xd[:], in_=ybf[:], func=AF.Square, accum_out=ssqp[:, 0:1]), reads=[B("ybf")], writes=[B("xd"), B("ssqp")])
                    for g_ in range(G):
                        for c0 in range(0, HPG * 64, 512):
                            cw = min(512, HPG * 64 - c0)
                            col0 = g_ * HPG * 64 + c0
                            hq0 = col0 // 64
                            nh = cw // 64
                            k2 = uc[0] % 2
                            uc[0] += 1
                            bHp, bHb = bHT[col0], bHTb[col0]
                            op("pe", lambda h, k2=k2, g_=g_, col0=col0, cw=cw: h.matmul(dps[k2][:, :cw], lhsT=Bt[:, g_ * 128:(g_ + 1) * 128], rhs=xdts[:, col0:col0 + cw], start=True, stop=True), reads=[bBt, B("xdts")], writes=[B("dps%d" % k2)])
                            op("dve", lambda h, col0=col0, cw=cw, hq0=hq0, nh=nh: h.tensor_tensor(out=HT[:, col0:col0 + cw].rearrange("p (h q) -> p h q", q=64), in0=HT[:, col0:col0 + cw].rearrange("p (h q) -> p h q", q=64), in1=sm[:, 3, hq0:hq0 + nh].unsqueeze(2).to_broadcast([128, nh, 64]), op=ALU.mult), reads=[bs], writes=[bHp])
                            op("dve", lambda h, k2=k2, col0=col0, cw=cw: h.tensor_tensor(out=HT[:, col0:col0 + cw], in0=HT[:, col0:col0 + cw], in1=dps[k2][:, :cw], op=ALU.add), reads=[B("dps%d" % k2)], writes=[bHp])
                            op("act", lambda h, col0=col0, cw=cw: h.copy(out=HTb[:, col0:col0 + cw], in_=HT[:, col0:col0 + cw]), reads=[bHp], writes=[bHb])
                    if final:
                        bq = B("ssqp")
                        op("dve", lambda h: h.reduce_sum(out=ssqp[:, NU:NU + 1], in_=ssqp[:, 0:NU], axis=AX.X), writes=[bq])
                        op("dve", lambda h: h.tensor_scalar(out=ssqp[:, NU + 1:NU + 2], in0=ssqp[:, NU:NU + 1], scalar1=1.0 / DI, scalar2=EPS, op0=ALU.mult, op1=ALU.add), writes=[bq])
                        op("act", lambda h: h.sqrt(out=ssqp[:, NU + 2:NU + 3], in_=ssqp[:, NU + 1:NU + 2]), writes=[bq])
                        op("dve", lambda h: h.reciprocal(out=rstdy[:, gchunk:gchunk + 1], in_=ssqp[:, NU + 2:NU + 3]), reads=[bq], writes=[B("rstdy")])
                        for q0 in range(0, NCD, 4):
                            pt, bp = tps[0], B("stps0")
                            gi_ = (q0 // 16) % 2
                            gs_, bgs = gstg[gi_], B("gstg%d" % gi_)

                            def tr(h, pt=pt, q0=q0):
                                ins = None
                                for q in range(4):
                                    ins = h.transpose(pt[:, q, :], ybf[:, (q0 + q) * 128:(q0 + q + 1) * 128], identb[:])
                                return ins
                            op("pe", tr, reads=[B("ybf"), bC], writes=[bp])
                            for q in range(4):
                                cq = q0 + q
                                if q % 2 == 0:
                                    op("act", lambda h, pt=pt, q=q, cq=cq, gs_=gs_: h.activation(out=gs_[:, cq % 16, :], in_=pt[:, q, :], func=AF.Identity, scale=gnc[:, cq:cq + 1]), reads=[bp, B("gnc")], writes=[bgs])
                                else:
                                    op("dve", lambda h, pt=pt, q=q, cq=cq, gs_=gs_: h.tensor_scalar(out=gs_[:, cq % 16, :], in0=pt[:, q, :], scalar1=gnc[:, cq:cq + 1], scalar2=None, op0=ALU.mult), reads=[bp, B("gnc")], writes=[bgs])
                            if (q0 + 4) % 16 == 0 or q0 + 4 >= NCD:
                                qa = (q0 // 16) * 16
                                qb = min(NCD, qa + 16)
                                load(gT[qa * 128:qb * 128, t0:t0 + 128].rearrange("(k p) t -> p k t", p=128), gs_[:, 0:qb - qa, :], reads=[bgs])

                def init_state(typ, d):
                    if typ == "p":
                        op("dve", lambda h: h.memset(HT[:], 0.0), writes=list(bHT.values()))
                        op("pool", lambda h: h.memset(HTb[:], 0.0), writes=list(bHTb.values()))
                    else:
                        for cq in range(NCD):
                            load(sts[:], st_in[d, cq * 128:(cq + 1) * 128, :], writes=[B("sts")])
                            op("pe", lambda h: h.transpose(fps[:], sts[:], identf[:]), reads=[B("sts"), bC], writes=[B("fps")])
                            op("dve", lambda h, cq=cq: h.tensor_copy(out=HT[:, cq * 128:(cq + 1) * 128], in_=fps[:]), reads=[B("fps")], writes=[bHT[(cq * 128 // pw) * pw]])
                        op("act", lambda h: h.copy(out=HTb[:], in_=HT[:]), reads=list(bHT.values()), writes=list(bHTb.values()))

                def out_state(si, d):
                    for cq in range(NCD):
                        op("pe", lambda h, cq=cq: h.transpose(fps[:], HT[:, cq * 128:(cq + 1) * 128], identf[:]), reads=[bHT[(cq * 128 // pw) * pw], bC], writes=[B("fps")])
                        op("dve", lambda h: h.tensor_copy(out=sts[:], in_=fps[:]), reads=[B("fps")], writes=[B("sts")])
                        load(ns_out[si, d, cq * 128:(cq + 1) * 128, :], sts[:], reads=[B("sts")])

                plan = []
                for (s0, L, typ, si) in seqs:
                    ncn = L // 128
                    plan.append(("init", typ, 1))
                    for cidx in reversed(range(ncn)):
                        plan.append(("chunk", s0, cidx, 1, False, (s0 // 128) + cidx))
                    if typ == "p":
                        plan.append(("out", si, 1))
                    plan.append(("init", typ, 0))
                    for cidx in range(ncn):
                        plan.append(("chunk", s0, cidx, 0, True, (s0 // 128) + cidx))
                    if typ == "p":
                        plan.append(("out", si, 0))
                chunks = [p for p in plan if p[0] == "chunk"]
                ci = 0
                chunk_loads(chunks[0][1], chunks[0][2], chunks[0][3], 0)
                for p in plan:
                    if p[0] == "init":
                        init_state(p[1], p[2])
                    elif p[0] == "out":
                        out_state(p[1], p[2])
                    else:
                        nx = chunks[ci + 1] if ci + 1 < len(chunks) else None
                        chunk(p[1], p[2], p[3], p[4], p[5], ci % 2, (nx[1], nx[2], nx[3]) if nx else None)
                        ci += 1
                sch.barrier()

        def phase_final():
            with ExitStack() as e7:
                fwb = sb("fwb", [128, D], F32, e7)
                hb = [sb("fhb%d" % i, [128, D], F32, e7) for i in range(2)]
                yo = [sb("fyo%d" % i, [128, D], F32, e7) for i in range(2)]
                junk = sb("fjunk", [128, D], BF16, e7)
                ssq = sb("fssq", [128, 4], F32, e7)
                e7.enter_context(nc.Block())
                load(fwb[:], fn_w.partition_broadcast(128), writes=[B("fwb")])
                bq = B("fssq")
                for tt in range(T // 128):
                    i2 = tt % 2
                    load(hb[i2][:], h2[tt * 128:(tt + 1) * 128, :], writes=[B("fhb%d" % i2)])
                    op("dve", lambda h: h.memset(ssq[:], 0.0), writes=[bq])
                    op("act", lambda h, i2=i2: h.activation(out=junk[:], in_=hb[i2][:], func=AF.Square, accum_out=ssq[:, 0:1]), reads=[B("fhb%d" % i2)], writes=[B("fjunk"), bq])
                    op("dve", lambda h: h.tensor_scalar(out=ssq[:, 1:2], in0=ssq[:, 0:1], scalar1=1.0 / D, scalar2=EPS, op0=ALU.mult, op1=ALU.add), writes=[bq])
                    op("act", lambda h: h.sqrt(out=ssq[:, 2:3], in_=ssq[:, 1:2]), writes=[bq])
                    op("dve", lambda h: h.reciprocal(out=ssq[:, 3:4], in_=ssq[:, 2:3]), writes=[bq])
                    op("dve", lambda h, i2=i2: h.scalar_tensor_tensor(out=yo[i2][:], in0=hb[i2][:], scalar=ssq[:, 3:4], in1=fwb[:], op0=ALU.mult, op1=ALU.mult), reads=[B("fhb%d" % i2), bq, B("fwb")], writes=[B("fyo%d" % i2)])
                    load(y_out[tt * 128:(tt + 1) * 128, :], yo[i2][:], reads=[B("fyo%d" % i2)])
                sch.barrier()

        phase_ada()
        phase_norm_inproj(0)
        phase_mid0()
        phase_outproj(0)
        phase_norm_inproj(1)
        phase_conv()
        phase_scan()
        phase_outproj(1)
        phase_final()
    return nc


_NC_CACHE = {}


def run_cfg(cfg, inputs, n_cores=8):
    c = cfg
    key = (c.D, c.NPS, c.SEQ, c.DSEQ)
    if key not in _NC_CACHE:
        _NC_CACHE[key] = build(c)
    nc = _NC_CACHE[key]
    f = lambda a: np.ascontiguousarray(np.asarray(a, dtype=np.float32))
    xp = f(inputs["x_prompt"])
    xs = f(inputs["x_sample"])
    st = f(inputs["state_ssd"])
    cc = f(inputs["c"])
    cctx = f(inputs["c_ctx"])
    nsamp = xs.shape[0]
    shared = {
        "ada_w": f(inputs["ada_w"]), "ada_b": f(inputs["ada_b"]), "norm_w": f(inputs["norm_w"]),
        "pool_in_w": f(inputs["pool_in_w"])[0], "pool_grp_w": f(inputs["pool_grp_w"])[0],
        "pool_grp_b": f(inputs["pool_grp_b"])[0].reshape(-1), "pool_scale": f(inputs["pool_scale"])[0],
        "pool_out_w": f(inputs["pool_out_w"])[0], "ssd_in_w": f(inputs["ssd_in_w"])[0],
        "ssd_conv_w": f(inputs["ssd_conv_w"])[0], "ssd_conv_b": f(inputs["ssd_conv_b"])[0],
        "ssd_dt_bias": f(inputs["ssd_dt_bias"])[0].reshape(-1), "ssd_A_log": f(inputs["ssd_A_log"])[0].reshape(-1),
        "ssd_D": f(inputs["ssd_D"])[0], "ssd_norm_w": f(inputs["ssd_norm_w"])[0],
        "ssd_out_w": f(inputs["ssd_out_w"])[0], "final_norm_w": f(inputs["final_norm_w"]),
    }
    in_maps = []
    for k in range(n_cores):
        sidx = k % nsamp
        m = dict(shared)
        m["x"] = np.concatenate([xp[k * c.NPS:(k + 1) * c.NPS].reshape(c.TP, c.D), xs[sidx]], axis=0)
        m["st"] = st[sidx, 0].reshape(2, c.H * c.P, c.N)
        m["cv"] = np.stack([cctx, cc[sidx]], axis=0)
        in_maps.append(m)
    res = run_bass_kernel_spmd(nc, in_maps, core_ids=list(range(n_cores)))
    rs = res.results
    yp = np.stack([rs[k]["y"][:c.TP].reshape(c.NPS, c.SEQ, c.D) for k in range(n_cores)], axis=0).reshape(n_cores * c.NPS, c.SEQ, c.D)
    ys = np.stack([rs[k]["y"][c.TP:] for k in range(nsamp)], axis=0)
    ns = np.concatenate([rs[k]["ns"] for k in range(n_cores)], axis=0).reshape(n_cores * c.NPS, 1, 2, c.H, c.P, c.N)
    return yp.astype(np.float32), ys.astype(np.float32), ns.astype(np.float32)


def kernel(**inputs):
    return run_cfg(Cfg(), inputs, 8)
```

```python
from contextlib import ExitStack
import numpy as np
import concourse.bass as bass
import concourse.mybir as mybir
from concourse.bass_utils import run_bass_kernel_spmd

F32, BF16 = mybir.dt.float32, mybir.dt.bfloat16
AF = mybir.ActivationFunctionType
ALU = mybir.AluOpType
AX = mybir.AxisListType
EPS = 1e-6
WINS = (2, 4, 8, 16)
USE_WCACHE = True


class Cfg:
    def __init__(self, D=4096, NPS=4, SEQ=256, DSEQ=2048):
        self.D, self.NPS, self.SEQ, self.DSEQ, self.GW = D, NPS, SEQ, DSEQ, 64
        self.E = 2 * D
        self.PGW = self.E // 4
        self.DI = 2 * D
        self.P = 64
        self.H = self.DI // 64
        self.N = 128
        self.G = 8
        self.HPG = self.H // self.G
        self.GN = self.G * self.N
        self.CONVCH = self.DI + 2 * self.GN
        self.SSD_IN = self.DI + self.CONVCH + 2 * self.H
        self.TP = NPS * SEQ
        self.T = self.TP + DSEQ
        self.TB = 512
        self.NB = self.T // self.TB
        self.KC = D // 128


class Buf:
    __slots__ = ("w", "rs")

    def __init__(self):
        self.w = None
        self.rs = []


class Stream:
    def __init__(self, name, h, sem):
        self.name, self.h, self.sem, self.cnt, self.seen = name, h, sem, 0, {}


class Queue:
    def __init__(self, name, stream, sems):
        self.name, self.stream, self.sems, self.cnt = name, stream, sems, 0


class Sched:
    def __init__(self):
        self.S = {}
        self.Q = {}
        self.bufs = {}

    def buf(self, key):
        b = self.bufs.get(key)
        if b is None:
            b = self.bufs[key] = Buf()
        return b

    def _wait(self, st, dep):
        kind, name, idx = dep
        if kind == "c":
            if name == st.name and name == "pe":
                return
            if st.seen.get(name, 0) >= idx:
                return
            st.h.wait_ge(self.S[name].sem, idx)
            st.seen[name] = idx
        else:
            q = self.Q[name]
            ns = len(q.sems)
            s = idx % ns
            val = 16 * (idx // ns + 1)
            key = (name, s)
            if st.seen.get(key, 0) >= val:
                return
            st.h.wait_ge(q.sems[s], val)
            st.seen[key] = val

    def op(self, sname, fn, reads=(), writes=(), q=None):
        st = self.S[sname]
        deps = set()
        for b in reads:
            if b.w is not None:
                deps.add(b.w)
        for b in writes:
            if b.w is not None:
                deps.add(b.w)
            deps.update(b.rs)
        qq = None
        if q is not None:
            qq = self.Q[q]
            if qq.cnt >= len(qq.sems):
                deps.add(("d", q, qq.cnt - len(qq.sems)))
        for d in sorted(deps):
            self._wait(st, d)
        ins = fn(st.h)
        if qq is None:
            st.cnt += 1
            ins.then_inc(st.sem, 1)
            me = ("c", sname, st.cnt)
        else:
            ins.then_inc(qq.sems[qq.cnt % len(qq.sems)], 16)
            me = ("d", q, qq.cnt)
            qq.cnt += 1
        for b in reads:
            b.rs.append(me)
        for b in writes:
            b.w = me
            b.rs = []
        return me

    def barrier(self):
        deps = []
        for s in self.S.values():
            if s.cnt:
                deps.append(("c", s.name, s.cnt))
        for q in self.Q.values():
            for i in range(max(0, q.cnt - len(q.sems)), q.cnt):
                deps.append(("d", q.name, i))
        for st in self.S.values():
            for d in deps:
                if d[0] == "c" and d[1] == st.name:
                    continue
                self._wait(st, d)
        for b in self.bufs.values():
            b.w = None
            b.rs = []


def build(c):
    D, E, DI, H, N, G, P, HPG, GN = c.D, c.E, c.DI, c.H, c.N, c.G, c.P, c.HPG, c.GN
    T, TB, NB, KC, TP = c.T, c.TB, c.NB, c.KC, c.TP
    CONVCH, SSD_IN, PGW = c.CONVCH, c.SSD_IN, c.PGW
    NTT = TB // 128
    nc = bass.Bass("TRN2", target_bir_lowering=False)

    def din(name, shape):
        return nc.dram_tensor(name, list(shape), F32, kind="ExternalInput").ap()

    def dscr(name, shape, dt):
        return nc.dram_tensor(name, list(shape), dt, kind="Internal").ap()

    x_in = din("x", [T, D])
    st_in = din("st", [2, H * P, N])
    cv_in = din("cv", [2, D])
    ada_w = din("ada_w", [2, D, 3 * D])
    ada_b = din("ada_b", [2, 3 * D])
    norm_w = din("norm_w", [2, D])
    pin_w = din("pool_in_w", [D, 2 * E])
    pgrp_w = din("pool_grp_w", [4, PGW, PGW])
    pgrp_b = din("pool_grp_b", [E])
    pscale = din("pool_scale", [E])
    pout_w = din("pool_out_w", [E, D])
    sin_w = din("ssd_in_w", [D, SSD_IN])
    conv_w = din("ssd_conv_w", [7, CONVCH])
    conv_b = din("ssd_conv_b", [CONVCH])
    dt_bias = din("ssd_dt_bias", [2 * H])
    a_log = din("ssd_A_log", [2 * H])
    d_skip = din("ssd_D", [H])
    gn_w = din("ssd_norm_w", [DI])
    sout_w = din("ssd_out_w", [DI, D])
    fn_w = din("final_norm_w", [D])
    y_out = nc.dram_tensor("y", [T, D], F32, kind="ExternalOutput").ap()
    ns_out = nc.dram_tensor("ns", [c.NPS, 2, H * P, N], F32, kind="ExternalOutput").ap()

    xp_tok = dscr("xp_tok", [T, E], BF16)
    zT = dscr("zT", [E, T], BF16)
    gT = dscr("gT", [E, T], BF16)
    h1 = dscr("h1", [T, D], F32)
    h2 = dscr("h2", [T, D], F32)
    z_tok = dscr("z_tok", [T, DI], BF16)
    xbc_pre = dscr("xbc_pre", [CONVCH, T], BF16)
    xbcT = dscr("xbcT", [CONVCH, T], BF16)
    xb_tok = dscr("xb_tok", [T, DI + GN], BF16)
    dt_tok = dscr("dt_tok", [T, 2 * H], F32)
    ybd = dscr("ybd", [T, DI], BF16)
    gb_d = dscr("gb_d", [2, 2, 128, D], F32)
    NCBM = max((2 * E + 511) // 512, (SSD_IN + 511) // 512)
    wcache = dscr("wcache", [NCBM, 128, KC, 512], BF16)
    wcache2 = dscr("wcache2", [(D // 512) * (E // 128 // 16), 128, 16, 512], BF16)

    sch = Sched()
    op = sch.op
    B = sch.buf
    es = ExitStack()
    with es:
        uid = [0]

        def sb(name, shape, dt, stack=None):
            uid[0] += 1
            return (stack or es).enter_context(nc.sbuf_tensor("%s_%d" % (name, uid[0]), list(shape), dt))

        def ps(name, shape, dt, stack=None):
            uid[0] += 1
            return (stack or es).enter_context(nc.psum_tensor("%s_%d" % (name, uid[0]), list(shape), dt))

        def sem(name):
            return es.enter_context(nc.semaphore(name))

        sems_c = {n: sem("c_" + n) for n in ("pe", "act", "dve", "pool", "sp")}
        qsp = [sem("qsp%d" % i) for i in range(16)]
        qpl = [sem("qpl%d" % i) for i in range(12)]
        sch.S = {"pe": Stream("pe", nc.tensor, sems_c["pe"]),
                 "act": Stream("act", nc.scalar, sems_c["act"]),
                 "dve": Stream("dve", nc.vector, sems_c["dve"]),
                 "pool": Stream("pool", nc.gpsimd, sems_c["pool"]),
                 "sp": Stream("sp", nc.sync, sems_c["sp"])}
        sch.Q = {"q_sp": Queue("q_sp", "sp", qsp), "q_pool": Queue("q_pool", "pool", qpl)}

        def load(out, in_, reads=(), writes=()):
            return op("sp", lambda h: h.dma_start(out=out, in_=in_), reads, writes, q="q_sp")

        def loadw(out, in_, reads=(), writes=()):
            return op("pool", lambda h: h.dma_start(out=out, in_=in_), reads, writes, q="q_pool")

        def wload(wb, pbufs, W2, r0, nk, c0, w, piece=8):
            for j, k0 in enumerate(range(0, nk, piece)):
                k1 = min(nk, k0 + piece)
                loadw(wb[:, k0:k1, :w], W2[r0 + k0 * 128:r0 + k1 * 128, c0:c0 + w].rearrange("(kc p) c -> p kc c", p=128), writes=[pbufs[j]])

        identf = sb("identf", [128, 128], F32)
        identb = sb("identb", [128, 128], BF16)
        onesf = sb("onesf", [128, 128], F32)
        onesb = sb("onesb", [128, 128], BF16)
        Lf = sb("Lf", [128, 128], F32)
        Lb = sb("Lb", [128, 128], F32)
        Uf = sb("Uf", [128, 128], F32)
        Ub = sb("Ub", [128, 128], F32)
        rstdy = sb("rstdy", [128, T // 128], F32)
        NCT = 3 * D // 128
        modc = [sb("modc%d" % i, [128, 2 * KC, 2], F32) for i in range(2)]
        Acol = [sb("Acol%d" % i, [128, KC, 2], F32) for i in range(2)]
        bC = B("consts")

        def mk_consts(h):
            h.memset(identf[:], 1.0)
            h.affine_select(out=identf[:], in_=identf[:], pattern=[[1, 128]], compare_op=ALU.is_equal, fill=0.0, base=0, channel_multiplier=-1)
            h.memset(onesf[:], 1.0)
            h.memset(onesb[:], 1.0)
            h.memset(Lf[:], 1.0)
            h.affine_select(out=Lf[:], in_=Lf[:], pattern=[[1, 128]], compare_op=ALU.is_ge, fill=0.0, base=0, channel_multiplier=-1)
            h.memset(Lb[:], 1.0)
            h.affine_select(out=Lb[:], in_=Lb[:], pattern=[[-1, 128]], compare_op=ALU.is_ge, fill=0.0, base=0, channel_multiplier=1)
            h.memset(Uf[:], 1.0)
            h.affine_select(out=Uf[:], in_=Uf[:], pattern=[[-1, 128]], compare_op=ALU.is_ge, fill=0.0, base=-1, channel_multiplier=1)
            h.memset(Ub[:], 1.0)
            return h.affine_select(out=Ub[:], in_=Ub[:], pattern=[[1, 128]], compare_op=ALU.is_ge, fill=0.0, base=-1, channel_multiplier=-1)

        def init_consts():
            op("pool", mk_consts, writes=[bC])
            op("dve", lambda h: h.tensor_copy(out=identb[:], in_=identf[:]), reads=[bC], writes=[bC])

        def mk_load_cols(stack, colps):
            colstage = sb("colstage", [128, 128], F32, stack)

            def load_cols(vec_ap, out_fn, n, bout):
                for i0 in range(0, n, 128):
                    m = min(128, n - i0)
                    load(colstage[:m, :], vec_ap[i0 * 128:(i0 + m) * 128].rearrange("(n p) -> n p", p=128), writes=[B("colstage")])
                    op("pe", lambda h: h.transpose(colps[:, :m], colstage[:m, :], identf[:m, :m]), reads=[B("colstage"), bC], writes=[B("colps")])
                    op("dve", lambda h: h.tensor_copy(out=out_fn(i0, m), in_=colps[:, :m]), reads=[B("colps")], writes=[bout])
            return load_cols

        def blk_cond(b):
            return 0 if b * TB < TP else 1

        def phase_ada():
            with ExitStack() as e1:
                gps = [ps("adagps%d" % j, [128, 512], F32, e1) for j in range(2)]
                aps = [ps("adaps%d" % j, [128, 4, 2], F32, e1) for j in range(2)]
                colps = ps("colps", [128, 128], F32, e1)
                load_cols = mk_load_cols(e1, colps)
                cvc = sb("cvc", [128, 2, KC], F32, e1)
                scol = sb("scol", [128, KC, 2], BF16, e1)
                scb = sb("scb", [128, KC, 2, 128], BF16, e1)
                adab = sb("adab", [128, 2, 2 * KC], F32, e1)
                nwc = sb("nwc", [128, 2, KC], F32, e1)
                abr = sb("abr", [128, 2, D], F32, e1)
                wbuf = [sb("adaw%d" % j, [128, KC, 512], BF16, e1) for j in range(2)]
                wpb = [[B("adaw%d_%d" % (j, k)) for k in range(KC // 8 + 1)] for j in range(2)]
                gst = [sb("gst%d" % j, [128, 512], F32, e1) for j in range(2)]
                e1.enter_context(nc.Block())
                init_consts()
                bcv = B("cvc")
                load_cols(cv_in.rearrange("a d -> (a d)"), lambda i0, m: cvc[:].rearrange("p a k -> p (a k)")[:, i0:i0 + m], 2 * KC, bcv)
                op("act", lambda h: h.activation(out=scol[:], in_=cvc[:].rearrange("p a k -> p k a"), func=AF.Silu), reads=[bcv], writes=[B("scol")])
                op("dve", lambda h: h.tensor_copy(out=scb[:], in_=scol[:].unsqueeze(3).to_broadcast([128, KC, 2, 128])), reads=[B("scol")], writes=[B("scb")])
                for i in range(2):
                    load_cols(ada_b[i, 0:2 * D], lambda i0, m, i=i: adab[:, i, i0:i0 + m], 2 * KC, B("adab"))
                    load_cols(norm_w[i], lambda i0, m, i=i: nwc[:, i, i0:i0 + m], KC, B("nwc"))
                    load(abr[:, i, :], ada_b[i, 2 * D:3 * D].partition_broadcast(128), writes=[B("abr")])
                blocks = [(i, cb) for i in range(2) for cb in range(3 * D // 512)]
                wload(wbuf[0], wpb[0], ada_w[0], 0, KC, 0, 512)
                gi = 0
                for it, (i, cb) in enumerate(blocks):
                    if it + 1 < len(blocks):
                        i2, cb2 = blocks[it + 1]
                        wload(wbuf[(it + 1) % 2], wpb[(it + 1) % 2], ada_w[i2], 0, KC, cb2 * 512, 512)
                    wb, bw = wbuf[it % 2], wpb[it % 2]
                    if cb < 2 * D // 512:
                        pp, bp = aps[it % 2], B("adaps%d" % (it % 2))

                        def mm(h, wb=wb, pp=pp):
                            ins = None
                            for ct in range(4):
                                for k in range(KC):
                                    ins = h.matmul(pp[:, ct, :], lhsT=wb[:, k, ct * 128:(ct + 1) * 128], rhs=scol[:, k, :], start=(k == 0), stop=(k == KC - 1))
                            return ins
                        op("pe", mm, reads=bw + [B("scol")], writes=[bp])
                        op("dve", lambda h, pp=pp, i=i, cb=cb: h.tensor_tensor(out=modc[i][:, cb * 4:(cb + 1) * 4, :], in0=pp[:], in1=adab[:, i, cb * 4:(cb + 1) * 4].unsqueeze(2).to_broadcast([128, 4, 2]), op=ALU.add), reads=[bp, B("adab")], writes=[B("modc")])
                    else:
                        c0 = cb * 512 - 2 * D
                        for cond in range(2):
                            pp, bp = gps[gi % 2], B("adagps%d" % (gi % 2))
                            gs, bg = gst[gi % 2], B("gst%d" % (gi % 2))
                            gi += 1

                            def mm(h, wb=wb, pp=pp, cond=cond):
                                ins = None
                                for k in range(KC):
                                    ins = h.matmul(pp[:], lhsT=scb[:, k, cond, :], rhs=wb[:, k, :], start=(k == 0), stop=(k == KC - 1))
                                return ins
                            op("pe", mm, reads=bw + [B("scb")], writes=[bp])
                            op("dve", lambda h, pp=pp, gs=gs, i=i, c0=c0: h.tensor_tensor(out=gs[:], in0=pp[:], in1=abr[:, i, c0:c0 + 512], op=ALU.add), reads=[bp, B("abr")], writes=[bg])
                            load(gb_d[i, cond, :, c0:c0 + 512], gs[:], reads=[bg])
                for i in range(2):
                    op("dve", lambda h, i=i: h.scalar_tensor_tensor(out=Acol[i][:], in0=modc[i][:, KC:2 * KC, :], scalar=1.0, in1=nwc[:, i, :].unsqueeze(2).to_broadcast([128, KC, 2]), op0=ALU.add, op1=ALU.mult), reads=[B("modc"), B("nwc")], writes=[B("Acol")])
                sch.barrier()

        def emit_norm(layer, b, hsrc, uT, bu, st, tts=None):
            cond = blk_cond(b)
            for tt in (range(NTT) if tts is None else tts):
                t0 = b * TB + tt * 128
                hb, bh = st["hb"][tt % 2], B("hb%d" % (tt % 2))
                load(hb[:], hsrc[t0:t0 + 128, :], writes=[bh])
                ssq, bq = st["ssq"], B("ssq")
                op("dve", lambda h: h.memset(ssq[:], 0.0), writes=[bq])
                op("act", lambda h, hb=hb: h.activation(out=st["junk"][:], in_=hb[:], func=AF.Square, accum_out=ssq[:, 0:1]), reads=[bh], writes=[B("junk"), bq])
                op("dve", lambda h: h.tensor_scalar(out=ssq[:, 1:2], in0=ssq[:, 0:1], scalar1=1.0 / D, scalar2=EPS, op0=ALU.mult, op1=ALU.add), writes=[bq])
                op("act", lambda h: h.sqrt(out=ssq[:, 2:3], in_=ssq[:, 1:2]), writes=[bq])
                op("dve", lambda h: h.reciprocal(out=ssq[:, 3:4], in_=ssq[:, 2:3]), writes=[bq])
                hn, bn = st["hn"][tt % 2], B("hn%d" % (tt % 2))
                op("dve", lambda h, hb=hb, hn=hn: h.tensor_scalar(out=hn[:], in0=hb[:], scalar1=ssq[:, 3:4], scalar2=None, op0=ALU.mult), reads=[bh, bq], writes=[bn])
                for k0 in range(0, KC, 4):
                    pt, bp = st["tps"][(k0 // 4) % 2], B("tps%d" % ((k0 // 4) % 2))

                    def tr(h, pt=pt, hn=hn, k0=k0):
                        ins = None
                        for kk in range(4):
                            ins = h.transpose(pt[:, kk, :], hn[:, (k0 + kk) * 128:(k0 + kk + 1) * 128], identb[:])
                        return ins
                    op("pe", tr, reads=[bn, bC], writes=[bp])
                    for kk in range(4):
                        k = k0 + kk
                        if kk % 2 == 0:
                            op("act", lambda h, pt=pt, kk=kk, k=k: h.activation(out=uT[:, k, tt * 128:(tt + 1) * 128], in_=pt[:, kk, :], func=AF.Identity, scale=Acol[layer][:, k, cond:cond + 1], bias=modc[layer][:, k, cond:cond + 1]), reads=[bp], writes=[bu])
                        else:
                            op("dve", lambda h, pt=pt, kk=kk, k=k: h.tensor_scalar(out=uT[:, k, tt * 128:(tt + 1) * 128], in0=pt[:, kk, :], scalar1=Acol[layer][:, k, cond:cond + 1], scalar2=modc[layer][:, k, cond:cond + 1], op0=ALU.mult, op1=ALU.add), reads=[bp], writes=[bu])

        def phase_norm_inproj(layer):
            with ExitStack() as e2:
                pst = [ps("gps%d" % i, [128, 512], F32, e2) for i in range(6)]
                st = {"hb": [sb("hb%d" % i, [128, D], F32, e2) for i in range(2)],
                      "hn": [sb("hn%d" % i, [128, D], BF16, e2) for i in range(2)],
                      "junk": sb("junk", [128, D], BF16, e2),
                      "ssq": sb("ssq", [128, 4], F32, e2),
                      "tps": [ps("tps%d" % i, [128, 4, 128], BF16, e2) for i in range(2)]}
                wbuf = [sb("inw%d" % i, [128, KC, 512], BF16, e2) for i in range(2)]
                wpb = [[B("inw%d_%d" % (i, k)) for k in range(KC // 8 + 1)] for i in range(2)]
                uTs = [sb("uT%d" % i, [128, KC, TB], BF16, e2) for i in range(2)]
                stg = [sb("stg%d" % i, [128, 512], BF16, e2) for i in range(3)]
                stgf = [sb("stgf%d" % i, [128, 512], F32, e2) for i in range(2)]
                cnt = {"p": 0, "s": 0, "f": 0}
                if layer == 1:
                    dtbb = sb("dtbb", [128, 2 * H], F32, e2)
                    spt = [sb("spt%d" % i, [128, 2 * H], F32, e2) for i in range(4)]
                e2.enter_context(nc.Block())
                if layer == 1:
                    load(dtbb[:], dt_bias.partition_broadcast(128), writes=[B("dtbb")])
                hsrc = x_in if layer == 0 else h1
                W = pin_w if layer == 0 else sin_w
                ncols = 2 * E if layer == 0 else SSD_IN
                cbs = [(c0, min(512, ncols - c0)) for c0 in range(0, ncols, 512)]

                def evac(b, c0, w, idx, pt, bp):
                    tsl = slice(b * TB + idx * 128, b * TB + (idx + 1) * 128)
                    if layer == 0 and c0 < E:
                        s, bs = stg[cnt["s"] % 3], B("stg%d" % (cnt["s"] % 3))
                        cnt["s"] += 1
                        op("act", lambda h: h.copy(out=s[:, :w], in_=pt[:, :w]), reads=[bp], writes=[bs])
                        load(xp_tok[tsl, c0:c0 + w], s[:, :w], reads=[bs])
                    elif layer == 0:
                        s, bs = stg[cnt["s"] % 3], B("stg%d" % (cnt["s"] % 3))
                        cnt["s"] += 1
                        op("act", lambda h: h.activation(out=s[:], in_=pt[:], func=AF.Silu), reads=[bp], writes=[bs])
                        r0 = c0 - E + idx * 128
                        load(zT[r0:r0 + 128, b * TB:(b + 1) * TB], s[:], reads=[bs])
                    elif c0 < DI:
                        s, bs = stg[cnt["s"] % 3], B("stg%d" % (cnt["s"] % 3))
                        cnt["s"] += 1
                        op("act", lambda h: h.activation(out=s[:, :w], in_=pt[:, :w], func=AF.Silu), reads=[bp], writes=[bs])
                        load(z_tok[tsl, c0:c0 + w], s[:, :w], reads=[bs])
                    elif c0 < DI + CONVCH:
                        s, bs = stg[cnt["s"] % 3], B("stg%d" % (cnt["s"] % 3))
                        cnt["s"] += 1
                        op("dve", lambda h: h.tensor_copy(out=s[:], in_=pt[:]), reads=[bp], writes=[bs])
                        r0 = c0 - DI + idx * 128
                        load(xbc_pre[r0:r0 + 128, b * TB:(b + 1) * TB], s[:], reads=[bs])
                    else:
                        t, bt = spt, B("spt")
                        W2 = 2 * H
                        op("dve", lambda h: h.tensor_tensor(out=t[0][:], in0=pt[:, :W2], in1=dtbb[:], op=ALU.add), reads=[bp, B("dtbb")], writes=[bt])
                        op("dve", lambda h: h.tensor_scalar(out=t[1][:], in0=t[0][:], scalar1=-1.0, scalar2=None, op0=ALU.mult), writes=[bt])
                        op("dve", lambda h: h.tensor_tensor(out=t[1][:], in0=t[1][:], in1=t[0][:], op=ALU.max), writes=[bt])
                        op("act", lambda h: h.activation(out=t[2][:], in_=t[1][:], func=AF.Exp, scale=-1.0), writes=[bt])
                        op("act", lambda h: h.activation(out=t[2][:], in_=t[2][:], func=AF.Ln, bias=1.0), writes=[bt])
                        op("dve", lambda h: h.tensor_scalar_max(out=t[1][:], in0=t[0][:], scalar1=0.0), writes=[bt])
                        op("dve", lambda h: h.tensor_tensor(out=t[3][:], in0=t[1][:], in1=t[2][:], op=ALU.add), writes=[bt])
                        load(dt_tok[tsl, :], t[3][:], reads=[bt])

                def is_tok(c0):
                    if layer == 0:
                        return c0 < E
                    return c0 < DI or c0 >= DI + CONVCH
                gidx = [(b, i) for b in range(NB) for i in range(len(cbs))]

                def get_w(n):
                    b_, i_ = gidx[n]
                    c0_, w_ = cbs[i_]
                    wb_, bw_ = wbuf[n % 2], wpb[n % 2]
                    if b_ == 0 or not USE_WCACHE:
                        wload(wb_, bw_, W, 0, KC, c0_, w_)
                        for j, k0 in enumerate(range(0, KC, 8)):
                            k1 = min(KC, k0 + 8)
                            if USE_WCACHE:
                                load(wcache[i_, :, k0:k1, :w_], wb_[:, k0:k1, :w_], reads=[bw_[j]], writes=[B("wc%d_%d" % (i_, j))])
                    else:
                        for j, k0 in enumerate(range(0, KC, 8)):
                            k1 = min(KC, k0 + 8)
                            load(wb_[:, k0:k1, :w_], wcache[i_, :, k0:k1, :w_], reads=[B("wc%d_%d" % (i_, j))], writes=[bw_[j]])
                get_w(0)
                emit_norm(layer, 0, hsrc, uTs[0], B("uT0"), st)
                for n, (b, i) in enumerate(gidx):
                    uT, bu = uTs[b % 2], B("uT%d" % (b % 2))
                    if b + 1 < NB and 1 <= i <= NTT:
                        emit_norm(layer, b + 1, hsrc, uTs[(b + 1) % 2], B("uT%d" % ((b + 1) % 2)), st, tts=[i - 1])
                    if n + 1 < len(gidx):
                        get_w(n + 1)
                    c0, w = cbs[i]
                    wb, bw = wbuf[n % 2], wpb[n % 2]
                    if is_tok(c0):
                        for tt in range(NTT):
                            pi = cnt["p"] % 6
                            cnt["p"] += 1
                            pt, bp = pst[pi], B("gps%d" % pi)

                            def mm(h, pt=pt, tt=tt, wb=wb, w=w):
                                ins = None
                                for k in range(KC):
                                    ins = h.matmul(pt[:, :w], lhsT=uT[:, k, tt * 128:(tt + 1) * 128], rhs=wb[:, k, :w], start=(k == 0), stop=(k == KC - 1))
                                return ins
                            op("pe", mm, reads=bw + [bu], writes=[bp])
                            evac(b, c0, w, tt, pt, bp)
                    else:
                        for ct in range(w // 128):
                            pi = cnt["p"] % 6
                            cnt["p"] += 1
                            pt, bp = pst[pi], B("gps%d" % pi)

                            def mm(h, pt=pt, ct=ct, wb=wb):
                                ins = None
                                for k in range(KC):
                                    ins = h.matmul(pt[:], lhsT=wb[:, k, ct * 128:(ct + 1) * 128], rhs=uT[:, k, :], start=(k == 0), stop=(k == KC - 1))
                                return ins
                            op("pe", mm, reads=bw + [bu], writes=[bp])
                            evac(b, c0, w, ct, pt, bp)
                sch.barrier()

        def phase_mid0():
            NG = PGW // 128
            rows = c.DSEQ // 64
            nts = c.DSEQ // 128
            with ExitStack() as e3:
                pps = [ps("pps%d" % i, [128, TB], F32, e3) for i in range(2)]
                gps_ = [ps("ggps%d" % i, [128, TB], F32, e3) for i in range(2)]
                cbps = ps("cbps", [128, 128], F32, e3)
                cps = ps("cps", [128, 2], F32, e3)
                colps = ps("colps", [128, 128], F32, e3)
                load_cols = mk_load_cols(e3, colps)
                psc = sb("psc", [128, E // 128], F32, e3)
                pgb = sb("pgb", [128, E // 128], F32, e3)
                pdl = [(w, d) for w in WINS for d in (-1, 0, 1)]
                PBt = sb("PBt", [128, len(pdl), 128], BF16, e3)
                Ft = sb("Ft", [128, 4, 128], BF16, e3)
                sdl = []
                for w in WINS:
                    for d in range(-5, 6):
                        if any(-w // 2 <= 2 * d + a - bb <= w // 2 - 1 for a in (0, 1) for bb in (0, 1)):
                            sdl.append((w, d))
                SBt = sb("SBt", [128, len(sdl), 128], BF16, e3)
                ncol = sb("ncol", [128, 1], F32, e3)
                Dt = {"p": sb("Dblk_p", [128, 4, 2, 128], BF16, e3), "s": sb("Dblk_s", [128, 4, nts, 128], BF16, e3)}
                It = {"p": sb("icnt_p", [128, 4, 2 * 128], F32, e3), "s": sb("icnt_s", [128, 4, nts * 128], F32, e3)}
                MAXI = 12
                xpl = [sb("xpl%d" % i, [128, MAXI, 512], BF16, e3) for i in range(2)]
                mixT = [sb("mixT%d" % i, [128, NG, TB], BF16, e3) for i in range(2)]
                gw = [sb("gw%d" % i, [128, NG, 512], BF16, e3) for i in range(3)]
                gwb = [[B("gw%d_%d" % (i, k)) for k in range(NG // 8 + 1)] for i in range(3)]
                szt = [sb("szt%d" % i, [128, TB], BF16, e3) for i in range(4)]
                t1 = [sb("t1_%d" % i, [128, TB], F32, e3) for i in range(2)]
                gto = [sb("gto%d" % i, [128, TB], BF16, e3) for i in range(2)]
                e3.enter_context(nc.Block())
                load_cols(pscale, lambda i0, m: psc[:, i0:i0 + m], E // 128, B("psc"))
                load_cols(pgrp_b, lambda i0, m: pgb[:, i0:i0 + m], E // 128, B("pgb"))
                op("dve", lambda h: h.tensor_tensor(out=pgb[:], in0=pgb[:], in1=psc[:], op=ALU.mult), reads=[B("psc")], writes=[B("pgb")])
                bK = B("poolconst")
                PBk = {}
                SBk = {}

                def band(h, ap, lo, hi, off):
                    h.memset(ap, 1.0)
                    h.affine_select(out=ap, in_=ap, pattern=[[-1, 128]], compare_op=ALU.is_ge, fill=0.0, base=off - lo, channel_multiplier=1)
                    return h.affine_select(out=ap, in_=ap, pattern=[[1, 128]], compare_op=ALU.is_ge, fill=0.0, base=hi - off, channel_multiplier=-1)

                def mkbands(h):
                    ins = None
                    for n, (w, d) in enumerate(pdl):
                        ins = band(h, PBt[:, n, :], -w // 2, w // 2 - 1, 128 * d)
                        PBk[(w, d)] = PBt[:, n, :]
                    for n, w in enumerate(WINS):
                        ins = band(h, Ft[:, n, :], -w // 2, w // 2 - 1, 0)
                    ins = h.memset(SBt[:], 0.0)
                    return ins
                op("pool", mkbands, writes=[bK])

                def mksb(h):
                    ins = None
                    for n, (w, d) in enumerate(sdl):
                        SBk[(w, d)] = SBt[:, n, :]
                        wi = WINS.index(w)
                        for a in (0, 1):
                            for bb in (0, 1):
                                if -w // 2 <= 2 * d + a - bb <= w // 2 - 1:
                                    ins = h.tensor_copy(out=SBt[64 * a:64 * a + 64, n, 64 * bb:64 * bb + 64], in_=Ft[64 * a:64 * a + 64, wi, 64 * a:64 * a + 64])
                    return ins
                op("dve", mksb, reads=[bK], writes=[bK])
                def pblk(w, i, j):
                    return PBk[(w, i - j)] if 0 <= i < 2 else None

                def sblk(w, i, j):
                    return SBk.get((w, i - j)) if 0 <= i < nts else None
                plans = {"p": (2, pblk), "s": (nts, sblk)}
                Dblk = {}
                icnt = {}
                for typ in ("p", "s"):
                    nt, bf = plans[typ]
                    dt_ = Dt[typ]
                    ic_ = It[typ]
                    for wi, w in enumerate(WINS):
                        for j in range(nt):
                            il = [i for i in range(j - 6, j + 7) if bf(w, i, j) is not None]

                            def cm(h, il=il, w=w, j=j, bf=bf):
                                for n, i in enumerate(il):
                                    h.matmul(cps[:, 0:1], lhsT=bf(w, i, j), rhs=onesb[:, 0:1], start=(n == 0), stop=(n == len(il) - 1))
                                ins = None
                                for n, i in enumerate(il):
                                    ins = h.matmul(cbps[:], lhsT=onesb[:], rhs=bf(w, i, j), start=(n == 0), stop=(n == len(il) - 1))
                                return ins
                            op("pe", cm, reads=[bK, bC], writes=[B("cps")])
                            op("dve", lambda h: h.tensor_scalar(out=ncol[:], in0=cps[:, 0:1], scalar1=-1.0, scalar2=None, op0=ALU.mult), reads=[B("cps")], writes=[B("ncol")])
                            op("dve", lambda h, dt_=dt_, wi=wi, j=j, w=w, bf=bf: h.scalar_tensor_tensor(out=dt_[:, wi, j, :], in0=identb[:], scalar=ncol[:, 0:1], in1=bf(w, j, j), op0=ALU.mult, op1=ALU.add), reads=[B("ncol"), bK], writes=[B("dblk")])
                            op("dve", lambda h, ic_=ic_, wi=wi, j=j: h.reciprocal(out=ic_[:, wi, j * 128:(j + 1) * 128], in_=cbps[:]), reads=[B("cps")], writes=[B("icnt"), B("cps")])
                            Dblk[(typ, w, j)] = dt_[:, wi, j, :]
                    icnt[typ] = ic_
                cn = {"x": 0, "p": 0, "m": 0, "w": 0, "g": 0}

                def blk_outs(b):
                    typ = "p" if b * TB < TP else "s"
                    outs = []
                    for jj in range(NTT):
                        at = b * NTT + jj
                        if typ == "p":
                            outs.append((at, (at // 2) * 2, at % 2))
                        else:
                            outs.append((at, TP // 128, at - TP // 128))
                    return typ, outs
                xitems = []
                szitems = []
                for b in range(NB):
                    typ, outs = blk_outs(b)
                    bf = plans[typ][1]
                    for gi, w in enumerate(WINS):
                        need = sorted({base + i for (at, base, j) in outs for i in range(j - 6, j + 7) if bf(w, i, j) is not None})
                        for sub in range(0, PGW, 512):
                            xitems.append((need, gi * PGW + sub, min(512, PGW - sub)))
                        for cb in range(0, PGW, 512):
                            for ct in range(min(512, PGW - cb) // 128):
                                szitems.append((b, gi * PGW + cb + ct * 128))

                def issue_x(m):
                    need, c0, sw = xitems[m]
                    assert need == list(range(need[0], need[-1] + 1)) and len(need) <= MAXI
                    load(xpl[m % 2][:, 0:len(need), :sw], xp_tok[need[0] * 128:(need[-1] + 1) * 128, c0:c0 + sw].rearrange("(t p) c -> p t c", p=128), writes=[B("xpl%d" % (m % 2))])

                def issue_sz(q):
                    b_, ech_ = szitems[q]
                    load(szt[q % 4][:], zT[ech_:ech_ + 128, b_ * TB:(b_ + 1) * TB], writes=[B("szt%d" % (q % 4))])
                issue_x(0)
                issue_sz(0)
                issue_sz(1)
                gwl = [(g2, cb2, min(512, PGW - cb2)) for _b in range(NB) for g2 in range(4) for cb2 in range(0, PGW, 512)]

                def issue_gw(m):
                    g2, cb2, cw2 = gwl[m]
                    wload(gw[m % 3], gwb[m % 3], pgrp_w[g2], 0, NG, cb2, cw2)
                issue_gw(0)
                issue_gw(1)
                for b in range(NB):
                    typ = "p" if b * TB < TP else "s"
                    nt, bf = plans[typ]
                    outs = []
                    for jj in range(NTT):
                        at = b * NTT + jj
                        if typ == "p":
                            outs.append((at, (at // 2) * 2, at % 2))
                        else:
                            outs.append((at, TP // 128, at - TP // 128))
                    for gi, w in enumerate(WINS):
                        need = sorted({base + i for (at, base, j) in outs for i in range(j - 6, j + 7) if bf(w, i, j) is not None})
                        assert len(need) <= MAXI
                        lidx = {a: n for n, a in enumerate(need)}
                        mx, bm = mixT[cn["m"] % 2], B("mixT%d" % (cn["m"] % 2))
                        cn["m"] += 1
                        for sub in range(0, PGW, 512):
                            sw = min(512, PGW - sub)
                            xl, bx = xpl[cn["x"] % 2], B("xpl%d" % (cn["x"] % 2))
                            assert xitems[cn["x"]][1] == gi * PGW + sub
                            if cn["x"] + 1 < len(xitems):
                                issue_x(cn["x"] + 1)
                            cn["x"] += 1
                            for ct in range(sw // 128):
                                pp, bp = pps[cn["p"] % 2], B("pps%d" % (cn["p"] % 2))
                                cn["p"] += 1

                                def pm(h, pp=pp, xl=xl, ct=ct, w=w, outs=outs, bf=bf, lidx=lidx, typ=typ):
                                    ins = None
                                    for jj, (at, base, j) in enumerate(outs):
                                        il = [i for i in range(j - 6, j + 7) if bf(w, i, j) is not None]
                                        for n, i in enumerate(il):
                                            blk = Dblk[(typ, w, j)] if i == j else bf(w, i, j)
                                            ins = h.matmul(pp[:, jj * 128:(jj + 1) * 128], lhsT=xl[:, lidx[base + i], ct * 128:(ct + 1) * 128], rhs=blk, start=(n == 0), stop=(n == len(il) - 1))
                                    return ins
                                op("pe", pm, reads=[bx, bK, B("dblk")], writes=[bp])
                                cti = sub // 128 + ct
                                if typ == "p":
                                    ic = icnt["p"][:, gi, :].unsqueeze(1).to_broadcast([128, TB // 256, 256])
                                    op("dve", lambda h, pp=pp, mx=mx, cti=cti, ic=ic: h.tensor_tensor(out=mx[:, cti, :].rearrange("p (s t) -> p s t", t=256), in0=pp[:].rearrange("p (s t) -> p s t", t=256), in1=ic, op=ALU.mult), reads=[bp, B("icnt")], writes=[bm])
                                else:
                                    j0 = outs[0][2]
                                    op("dve", lambda h, pp=pp, mx=mx, cti=cti, j0=j0, gi=gi: h.tensor_tensor(out=mx[:, cti, :], in0=pp[:], in1=icnt["s"][:, gi, j0 * 128:j0 * 128 + TB], op=ALU.mult), reads=[bp, B("icnt")], writes=[bm])
                        for cb in range(0, PGW, 512):
                            cw = min(512, PGW - cb)
                            wb, bw = gw[cn["w"] % 3], gwb[cn["w"] % 3]
                            assert gwl[cn["w"]] == (gi, cb, cw)
                            if cn["w"] + 2 < len(gwl):
                                issue_gw(cn["w"] + 2)
                            cn["w"] += 1
                            for ct in range(cw // 128):
                                ech = gi * PGW + cb + ct * 128
                                i2 = cn["g"] % 2
                                i4 = cn["g"] % 4
                                assert szitems[cn["g"]] == (b, ech)
                                if cn["g"] + 2 < len(szitems):
                                    issue_sz(cn["g"] + 2)
                                cn["g"] += 1
                                pp, bp = gps_[i2], B("ggps%d" % i2)

                                def gm(h, pp=pp, wb=wb, ct=ct, mx=mx):
                                    ins = None
                                    for k in range(NG):
                                        ins = h.matmul(pp[:], lhsT=wb[:, k, ct * 128:(ct + 1) * 128], rhs=mx[:, k, :], start=(k == 0), stop=(k == NG - 1))
                                    return ins
                                op("pe", gm, reads=bw + [bm], writes=[bp])
                                ec = ech // 128
                                op("act", lambda h, pp=pp, i2=i2, ec=ec: h.activation(out=t1[i2][:], in_=pp[:], func=AF.Identity, scale=psc[:, ec:ec + 1], bias=pgb[:, ec:ec + 1]), reads=[bp, B("pgb")], writes=[B("t1_%d" % i2)])
                                op("dve", lambda h, i2=i2, i4=i4: h.tensor_tensor(out=gto[i2][:], in0=t1[i2][:], in1=szt[i4][:], op=ALU.mult), reads=[B("t1_%d" % i2), B("szt%d" % i4)], writes=[B("gto%d" % i2)])
                                load(gT[ech:ech + 128, b * TB:(b + 1) * TB], gto[i2][:], reads=[B("gto%d" % i2)])
                sch.barrier()

        def phase_outproj(layer):
            Wm = pout_w if layer == 0 else sout_w
            hsrc = x_in if layer == 0 else h1
            hdst = h1 if layer == 0 else h2
            KE = E // 128
            KQ = 16
            with ExitStack() as e4:
                pst = [ps("ops%d" % i, [128, 512], F32, e4) for i in range(8)]
                gTb, bg = sb("gTb", [128, KE, TB], BF16, e4), B("gTb")
                Gb = sb("Gb", [128, 2, D], F32, e4)
                wq = [sb("wq%d" % i, [128, KQ, 512], BF16, e4) for i in range(4)]
                wqb = [[B("wq%d_%d" % (i, k)) for k in range(KQ // 8 + 1)] for i in range(4)]
                ht = [sb("ht%d" % i, [128, 512], F32, e4) for i in range(8)]
                tt_ = [sb("ot%d" % i, [128, 512], F32, e4) for i in range(3)]
                e4.enter_context(nc.Block())
                for cond in range(2):
                    load(Gb[:, cond, :], gb_d[layer, cond], writes=[B("Gb")])
                pieces = [(b, cb, kq) for b in range(NB) for cb in range(D // 512) for kq in range(KE // KQ)]

                def issue(n):
                    b, cb, kq = pieces[n]
                    pid = cb * (KE // KQ) + kq
                    wb_, bw_ = wq[n % 4], wqb[n % 4]
                    if b == 0:
                        wload(wb_, bw_, Wm, kq * KQ * 128, KQ, cb * 512, 512)
                        for j, k0 in enumerate(range(0, KQ, 8)):
                            load(wcache2[pid, :, k0:k0 + 8, :], wb_[:, k0:k0 + 8, :], reads=[bw_[j]], writes=[B("wc2_%d_%d" % (pid, j))])
                    else:
                        for j, k0 in enumerate(range(0, KQ, 8)):
                            load(wb_[:, k0:k0 + 8, :], wcache2[pid, :, k0:k0 + 8, :], reads=[B("wc2_%d_%d" % (pid, j))], writes=[bw_[j]])
                for n in range(min(3, len(pieces))):
                    issue(n)
                pcnt = 0
                hc = 0
                for n, (b, cb, kq) in enumerate(pieces):
                    cond = blk_cond(b)
                    if cb == 0 and kq == 0:
                        for k0 in range(0, KE, 16):
                            load(gTb[:, k0:k0 + 16, :], gT[k0 * 128:(k0 + 16) * 128, b * TB:(b + 1) * TB].rearrange("(k p) t -> p k t", p=128), writes=[bg] if k0 == 0 else [B("gTb_%d" % k0)])
                    if n + 3 < len(pieces):
                        issue(n + 3)
                    if kq == 0:
                        base = pcnt
                        pcnt += NTT
                        hbase = hc
                        for tt in range(NTT):
                            load(ht[(hbase + tt) % 8][:], hsrc[b * TB + tt * 128:b * TB + (tt + 1) * 128, cb * 512:(cb + 1) * 512], writes=[B("ht%d" % ((hbase + tt) % 8))])
                        hc += NTT
                    wb, bw = wq[n % 4], wqb[n % 4]
                    for tt in range(NTT):
                        pi = (base + tt) % 8
                        pt, bp = pst[pi], B("ops%d" % pi)

                        def mm(h, pt=pt, tt=tt, wb=wb, kq=kq):
                            ins = None
                            for k in range(KQ):
                                kk = kq * KQ + k
                                ins = h.matmul(pt[:], lhsT=gTb[:, kk, tt * 128:(tt + 1) * 128], rhs=wb[:, k, :], start=(kk == 0), stop=(kk == KE - 1))
                            return ins
                        op("pe", mm, reads=bw + [bg] + [B("gTb_%d" % k0) for k0 in range(16, KE, 16)], writes=[bp] if kq == 0 else [], )
                        if kq != 0:
                            bp.w = ("c", "pe", sch.S["pe"].cnt)
                        if kq == KE // KQ - 1:
                            tsl = slice(b * TB + tt * 128, b * TB + (tt + 1) * 128)
                            csl = slice(cb * 512, (cb + 1) * 512)
                            i3 = (hbase + tt) % 3
                            i8 = (hbase + tt) % 8
                            if layer == 0:
                                op("dve", lambda h, pt=pt, i3=i3, cond=cond, csl=csl: h.tensor_tensor(out=tt_[i3][:], in0=pt[:], in1=Gb[:, cond, csl], op=ALU.mult), reads=[bp, B("Gb")], writes=[B("ot%d" % i3)])
                            else:
                                ti = (b * TB + tt * 128) // 128
                                op("dve", lambda h, pt=pt, i3=i3, cond=cond, csl=csl, ti=ti: h.scalar_tensor_tensor(out=tt_[i3][:], in0=pt[:], scalar=rstdy[:, ti:ti + 1], in1=Gb[:, cond, csl], op0=ALU.mult, op1=ALU.mult), reads=[bp, B("Gb")], writes=[B("ot%d" % i3)])
                            op("pool", lambda h, i3=i3, i8=i8: h.tensor_tensor(out=tt_[i3][:], in0=tt_[i3][:], in1=ht[i8][:], op=ALU.add), reads=[B("ht%d" % i8)], writes=[B("ot%d" % i3)])
                            load(hdst[tsl, csl], tt_[i3][:], reads=[B("ot%d" % i3)])
                sch.barrier()

        seqs = [(i * c.SEQ, c.SEQ, "p", i) for i in range(c.NPS)] + [(TP, c.DSEQ, "s", 0)]

        def phase_conv():
            NCH = CONVCH // 128
            LM = max(c.SEQ, c.DSEQ)
            with ExitStack() as e5:
                cvps = [ps("cvps%d" % i, [128, 512], F32, e5) for i in range(4)]
                tps = [ps("ctps%d" % i, [128, 4, 128], BF16, e5) for i in range(2)]
                colps = ps("colps", [128, 128], F32, e5)
                load_cols = mk_load_cols(e5, colps)
                cwc = sb("cwc", [128, 7, NCH], F32, e5)
                cbc = sb("cbc", [128, NCH], F32, e5)
                pre = [sb("pre%d" % i, [128, LM + 6], BF16, e5) for i in range(3)]
                dg = [sb("dg%d" % i, [128, 7, 128], BF16, e5) for i in range(2)]
                post = [sb("post%d" % i, [128, LM], BF16, e5) for i in range(2)]
                tst = [sb("tst%d" % i, [128, LM // 128, 128], BF16, e5) for i in range(2)]
                e5.enter_context(nc.Block())
                for k in range(7):
                    load_cols(conv_w[k], lambda i0, m, k=k: cwc[:, k, i0:i0 + m], NCH, B("cwc"))
                load_cols(conv_b, lambda i0, m: cbc[:, i0:i0 + m], NCH, B("cwc"))
                for i in range(3):
                    op("pool", lambda h, i=i: h.memset(pre[i][:], 0.0), writes=[B("pre%d" % i)])
                tc = 0
                pc = 0
                items = [(s0, L, ct) for (s0, L, typ, si) in seqs for ct in range(NCH)]

                def issue_pre(n):
                    s0, L, ct = items[n]
                    load(pre[n % 3][:, 3:3 + L], xbc_pre[ct * 128:(ct + 1) * 128, s0:s0 + L], writes=[B("pre%d" % (n % 3))])
                issue_pre(0)
                issue_pre(1)

                def mk_dg(n):
                    ct_ = items[n][2]
                    op("dve", lambda h: h.tensor_tensor(out=dg[n % 2][:], in0=identb[:].unsqueeze(1).to_broadcast([128, 7, 128]), in1=cwc[:, :, ct_:ct_ + 1].to_broadcast([128, 7, 128]), op=ALU.mult), reads=[B("cwc"), bC], writes=[B("dg%d" % (n % 2))])
                for n, (s0, L, ct) in enumerate(items):
                    if True:
                        if n + 2 < len(items):
                            issue_pre(n + 2)
                        i2 = n % 2
                        i3 = n % 3
                        pr, po, dgi = pre[i3], post[i2], dg[i2]
                        bpr, bpo, bdg = B("pre%d" % i3), B("post%d" % i2), B("dg%d" % i2)
                        if n == 0:
                            mk_dg(0)
                        if n + 1 < len(items):
                            mk_dg(n + 1)
                        for q0 in range(0, L, 512):
                            qw = min(512, L - q0)
                            cp, bcp = cvps[pc % 4], B("cvps%d" % (pc % 4))
                            pc += 1

                            def cm(h, cp=cp, pr=pr, dgi=dgi, q0=q0, qw=qw):
                                ins = None
                                for k in range(7):
                                    ins = h.matmul(cp[:, :qw], lhsT=dgi[:, k, :], rhs=pr[:, q0 + k:q0 + k + qw], start=(k == 0), stop=(k == 6))
                                return ins
                            op("pe", cm, reads=[bpr, bdg], writes=[bcp])
                            op("act", lambda h, cp=cp, po=po, q0=q0, qw=qw, ct=ct: h.activation(out=po[:, q0:q0 + qw], in_=cp[:, :qw], func=AF.Silu, bias=cbc[:, ct:ct + 1]), reads=[bcp, B("cwc")], writes=[bpo])
                        load(xbcT[ct * 128:(ct + 1) * 128, s0:s0 + L], po[:, :L], reads=[bpo])
                        if ct < (DI + GN) // 128:
                            ts_, bts = tst[i2], B("tst%d" % i2)
                            for q0 in range(0, L // 128, 4):
                                pt, bp = tps[tc % 2], B("ctps%d" % (tc % 2))
                                tc += 1
                                nq = min(4, L // 128 - q0)

                                def tr(h, pt=pt, po=po, q0=q0, nq=nq):
                                    ins = None
                                    for q in range(nq):
                                        ins = h.transpose(pt[:, q, :], po[:, (q0 + q) * 128:(q0 + q + 1) * 128], identb[:])
                                    return ins
                                op("pe", tr, reads=[bpo, bC], writes=[bp])
                                op("dve", lambda h, pt=pt, ts_=ts_, q0=q0, nq=nq: h.tensor_copy(out=ts_[:, q0:q0 + nq, :], in_=pt[:, :nq, :]), reads=[bp], writes=[bts])
                            load(xb_tok[s0:s0 + L, ct * 128:(ct + 1) * 128].rearrange("(t p) c -> p t c", p=128), ts_[:, :L // 128, :], reads=[bts])
                sch.barrier()

        def phase_scan():
            NU = H // 4
            NCD = DI // 128
            with ExitStack() as e6:
                dps = [ps("dps%d" % i, [128, 512], F32, e6) for i in range(2)]
                yps = [ps("yps%d" % i, [128, 2, 256], F32, e6) for i in range(2)]
                scps = ps("scps", [128, G // 2, 128], F32, e6)
                aps_ = ps("aps", [128, 2, H], F32, e6)
                tps = [ps("stps%d" % i, [128, 4, 128], BF16, e6) for i in range(1)]
                fps = ps("fps", [128, 128], F32, e6)
                load_cols = mk_load_cols(e6, fps)
                gnc = sb("gnc", [128, NCD], F32, e6)
                Ab = sb("Ab", [128, 2 * H], F32, e6)
                Db = sb("Db", [128, H], F32, e6)
                xt = [sb("xt%d" % i, [128, DI], BF16, e6) for i in range(1)] * 2
                xd = sb("xd", [128, DI], BF16, e6)
                bt_ = [sb("bt%d" % i, [128, GN], BF16, e6) for i in range(2)]
                BTt = [sb("BT%d" % i, [128, G, 128], BF16, e6) for i in range(2)]
                CTt = [sb("CT%d" % i, [128, G, 128], BF16, e6) for i in range(2)]
                dtt = [sb("dtt%d" % i, [128, H], F32, e6) for i in range(2)]
                zt = sb("zt", [128, DI], BF16, e6)
                HT = sb("HT", [128, DI], F32, e6)
                HTb = sb("HTb", [128, DI], BF16, e6)
                xdt = sb("xdt", [128, DI], BF16, e6)
                xdts = sb("xdts", [128, DI], BF16, e6)
                ybf = sb("ybf", [128, DI], BF16, e6)
                gstg = [sb("gstg%d" % i, [128, 16, 128], BF16, e6) for i in range(2)]
                ybch = sb("ybch", [128, DI], BF16, e6)
                sm = sb("sm", [128, 6, H], F32, e6)
                scm = sb("scm", [128, G, 128], BF16, e6)
                ssqp = sb("ssqp", [128, NU + 4], F32, e6)
                junk = sb("sjunk", [128, 256], F32, e6)
                rhsb = [sb("rhsb%d" % i, [128, 4, 128], F32, e6) for i in range(2)]
                Eb = [sb("Eb%d" % i, [128, 4, 128], BF16, e6) for i in range(2)]
                MT = [sb("MT%d" % i, [128, 4, 128], BF16, e6) for i in range(2)]
                yu = [sb("yu%d" % i, [128, 256], F32, e6) for i in range(3)]
                sts = sb("sts", [128, 128], F32, e6)
                e6.enter_context(nc.Block())
                load_cols(gn_w, lambda i0, m: gnc[:, i0:i0 + m], NCD, B("gnc"))
                load(Ab[:], a_log.partition_broadcast(128), writes=[B("Ab")])
                load(Db[:], d_skip.partition_broadcast(128), writes=[B("Db")])
                op("act", lambda h: h.activation(out=Ab[:], in_=Ab[:], func=AF.Exp), writes=[B("Ab")])
                op("dve", lambda h: h.tensor_scalar(out=Ab[:], in0=Ab[:], scalar1=-1.0, scalar2=None, op0=ALU.mult), writes=[B("Ab")])
                uc = [0]
                pw = min(512, HPG * 64)
                bHT = {c0: B("HT_%d" % c0) for c0 in range(0, DI, pw)}
                bHTb = {c0: B("HTb_%d" % c0) for c0 in range(0, DI, pw)}

                def chunk_loads(s0, cidx, d, i2):
                    t0 = s0 + cidx * 128
                    load(xt[0][:], xb_tok[t0:t0 + 128, 0:DI], writes=[B("xt0")])
                    load(bt_[i2][:], xb_tok[t0:t0 + 128, DI:DI + GN], writes=[B("bt%d" % i2)])
                    load(BTt[i2][:], xbcT[DI:DI + GN, t0:t0 + 128].rearrange("(g n) t -> n g t", n=128), writes=[B("BT%d" % i2)])
                    load(CTt[i2][:], xbcT[DI + GN:DI + 2 * GN, t0:t0 + 128].rearrange("(g n) t -> n g t", n=128), writes=[B("CT%d" % i2)])
                    load(dtt[i2][:], dt_tok[t0:t0 + 128, d * H:(d + 1) * H], writes=[B("dtt%d" % i2)])

                def chunk(s0, cidx, d, final, gchunk, i2, nxt):
                    t0 = s0 + cidx * 128
                    X, bX = xt[0], B("xt0")
                    Bt, bBt = bt_[i2], B("bt%d" % i2)
                    BT, bBT = BTt[i2], B("BT%d" % i2)
                    CT, bCT = CTt[i2], B("CT%d" % i2)
                    dtc, bdt = dtt[i2], B("dtt%d" % i2)
                    Ld, Ud = (Lf, Uf) if d == 0 else (Lb, Ub)
                    if final:
                        load(zt[:], z_tok[t0:t0 + 128, :], writes=[B("zt")])
                    bs = B("sm")
                    op("dve", lambda h: h.tensor_tensor(out=sm[:, 0, :], in0=dtc[:], in1=Ab[:, d * H:(d + 1) * H], op=ALU.mult), reads=[bdt, B("Ab")], writes=[bs])

                    def am(h):
                        h.matmul(aps_[:, 0, :], lhsT=Ld[:], rhs=sm[:, 0, :], start=True, stop=True)
                        return h.matmul(aps_[:, 1, :], lhsT=onesf[:], rhs=sm[:, 0, :], start=True, stop=True)
                    op("pe", am, reads=[bs, bC], writes=[B("aps")])
                    op("act", lambda h: h.activation(out=sm[:, 1, :], in_=aps_[:, 0, :], func=AF.Exp), reads=[B("aps")], writes=[bs])
                    op("dve", lambda h: h.tensor_copy(out=sm[:, 5, :], in_=aps_[:, 0, :]), reads=[B("aps")], writes=[bs])
                    op("dve", lambda h: h.tensor_tensor(out=sm[:, 2, :], in0=aps_[:, 1, :], in1=sm[:, 5, :], op=ALU.subtract), reads=[B("aps")], writes=[bs])
                    op("act", lambda h: h.activation(out=sm[:, 2, :], in_=sm[:, 2, :], func=AF.Exp), writes=[bs])
                    op("act", lambda h: h.activation(out=sm[:, 3, :], in_=aps_[:, 1, :], func=AF.Exp), reads=[B("aps")], writes=[bs])
                    op("dve", lambda h: h.tensor_tensor(out=sm[:, 4, :], in0=dtc[:], in1=sm[:, 2, :], op=ALU.mult), writes=[bs])
                    op("dve", lambda h: h.tensor_tensor(out=xdt[:].rearrange("p (h q) -> p h q", q=64), in0=X[:].rearrange("p (h q) -> p h q", q=64), in1=dtc[:].unsqueeze(2).to_broadcast([128, H, 64]), op=ALU.mult), reads=[bX, bdt], writes=[B("xdt")])
                    op("dve", lambda h: h.tensor_tensor(out=xdts[:].rearrange("p (h q) -> p h q", q=64), in0=X[:].rearrange("p (h q) -> p h q", q=64), in1=sm[:, 4, :].unsqueeze(2).to_broadcast([128, H, 64]), op=ALU.mult), reads=[bX, bs], writes=[B("xdts")])
                    for gh in range(2):
                        def smm(h, gh=gh):
                            ins = None
                            for gg in range(G // 2):
                                g_ = gh * (G // 2) + gg
                                ins = h.matmul(scps[:, gg, :], lhsT=BT[:, g_, :], rhs=CT[:, g_, :], start=True, stop=True)
                            return ins
                        op("pe", smm, reads=[bBT, bCT], writes=[B("scps")])
                        op("dve", lambda h, gh=gh: h.tensor_tensor(out=scm[:, gh * (G // 2):(gh + 1) * (G // 2), :], in0=scps[:], in1=Ld[:].unsqueeze(1).to_broadcast([128, G // 2, 128]), op=ALU.mult), reads=[B("scps"), bC], writes=[B("scm")])
                    if final:
                        op("dve", lambda h: h.memset(ssqp[:], 0.0), writes=[B("ssqp")])
                        load(ybch[:], ybd[t0:t0 + 128, :], reads=[B("ybd%d" % gchunk)], writes=[B("ybch")])
                        op("dve", lambda h: h.tensor_tensor(out=xd[:].rearrange("p (h q) -> p h q", q=64), in0=X[:].rearrange("p (h q) -> p h q", q=64), in1=Db[:].unsqueeze(2).to_broadcast([128, H, 64]), op=ALU.mult), reads=[bX, B("Db")], writes=[B("xd")])
                    ubase = uc[0]
                    if nxt is not None:
                        chunk_loads(nxt[0], nxt[1], nxt[2], 1 - i2)

                    def S1(u):
                        h0 = 4 * u
                        k2 = (ubase + u) % 2
                        op("dve", lambda h, k2=k2, h0=h0: h.tensor_tensor(out=rhsb[k2][:], in0=sm[:, 0, h0:h0 + 4].unsqueeze(2).to_broadcast([128, 4, 128]), in1=Ld[:].unsqueeze(1).to_broadcast([128, 4, 128]), op=ALU.mult), reads=[bs, bC], writes=[B("rhsb%d" % k2)])
                        op("pe", lambda h, k2=k2: h.matmul(dps[k2][:], lhsT=Ud[:], rhs=rhsb[k2][:].rearrange("p a b -> p (a b)"), start=True, stop=True), reads=[B("rhsb%d" % k2), bC], writes=[B("dps%d" % k2)])
                        op("act", lambda h, k2=k2: h.activation(out=Eb[k2][:].rearrange("p a b -> p (a b)"), in_=dps[k2][:], func=AF.Exp), reads=[B("dps%d" % k2)], writes=[B("Eb%d" % k2)])

                    def S2(u):
                        h0 = 4 * u
                        g_ = h0 // HPG
                        k2 = (ubase + u) % 2
                        k3 = u % 3
                        Y, bY = yu[k3], B("yu%d" % k3)
                        csl = slice(h0 * 64, h0 * 64 + 256)
                        op("dve", lambda h, k2=k2, g_=g_: h.tensor_tensor(out=MT[k2][:], in0=Eb[k2][:], in1=scm[:, g_, :].unsqueeze(1).to_broadcast([128, 4, 128]), op=ALU.mult), reads=[B("Eb%d" % k2), B("scm")], writes=[B("MT%d" % k2)])

                        def ym(h, k2=k2, h0=h0, g_=g_):
                            if final:
                                h.matmul(yps[k2][:, 0, :], lhsT=identb[:], rhs=ybch[:, csl], start=True, stop=False)
                                h.matmul(yps[k2][:, 0, :], lhsT=identb[:], rhs=xd[:, csl], start=False, stop=False, skip_group_check=True)
                            for hh in range(4):
                                h.matmul(yps[k2][:, 0, hh * 64:(hh + 1) * 64], lhsT=MT[k2][:, hh, :], rhs=xdt[:, (h0 + hh) * 64:(h0 + hh + 1) * 64], start=(not final), stop=True, skip_group_check=True)
                            return h.matmul(yps[k2][:, 1, :], lhsT=CT[:, g_, :], rhs=HTb[:, h0 * 64:h0 * 64 + 256], start=True, stop=True)
                        rds = [B("MT%d" % k2), B("xdt"), bCT, bHTb[(h0 * 64 // pw) * pw], bC] + ([B("ybch"), B("xd")] if final else [])
                        op("pe", ym, reads=rds, writes=[B("yps%d" % k2)])

                        def sc(h, k2=k2, Y=Y, h0=h0):
                            ins = None
                            for hh in range(4):
                                ins = h.activation(out=Y[:, hh * 64:(hh + 1) * 64], in_=yps[k2][:, 1, hh * 64:(hh + 1) * 64], func=AF.Identity, scale=sm[:, 1, h0 + hh:h0 + hh + 1])
                            return ins
                        op("act", sc, reads=[B("yps%d" % k2), bs], writes=[bY])

                    def S3(u):
                        h0 = 4 * u
                        k2 = (ubase + u) % 2
                        k3 = u % 3
                        Y, bY = yu[k3], B("yu%d" % k3)
                        csl = slice(h0 * 64, h0 * 64 + 256)
                        if not final:
                            op("dve", lambda h, k2=k2, Y=Y: h.tensor_tensor(out=ybch[:, csl], in0=Y[:], in1=yps[k2][:, 0, :], op=ALU.add), reads=[B("yps%d" % k2), bY], writes=[B("ybch")])
                        else:
                            op("dve", lambda h, k2=k2, Y=Y: h.tensor_tensor(out=Y[:], in0=Y[:], in1=yps[k2][:, 0, :], op=ALU.add), reads=[B("yps%d" % k2)], writes=[bY])

                    def S4(u):
                        h0 = 4 * u
                        k3 = u % 3
                        Y, bY = yu[k3], B("yu%d" % k3)
                        csl = slice(h0 * 64, h0 * 64 + 256)
                        op("dve", lambda h, Y=Y: h.tensor_tensor(out=ybf[:, csl], in0=Y[:], in1=zt[:, csl], op=ALU.mult), reads=[B("zt"), bY], writes=[B("ybf")])
                    for i in range(-2, NU + 1):
                        if 0 <= i + 2 < NU:
                            S1(i + 2)
                        if 0 <= i + 1 < NU:
                            S2(i + 1)
                        if 0 <= i < NU:
                            S3(i)
                        if final and 0 <= i - 1 < NU:
                            S4(i - 1)
                    uc[0] += NU
                    if not final:
                        load(ybd[t0:t0 + 128, :], ybch[:], reads=[B("ybch")], writes=[B("ybd%d" % gchunk)])
                    else:
                        op("act", lambda h: h.activation(out=xd[:], in_=ybf[:], func=AF.Square, accum_out=ssqp[:, 0:1]), reads=[B("ybf")], writes=[B("xd"), B("ssqp")])
                    for g_ in range(G):
                        for c0 in range(0, HPG * 64, 512):
                            cw = min(512, HPG * 64 - c0)
                            col0 = g_ * HPG * 64 + c0
                            hq0 = col0 // 64
                            nh = cw // 64
                            k2 = uc[0] % 2
                            uc[0] += 1
                            bHp, bHb = bHT[col0], bHTb[col0]
                            op("pe", lambda h, k2=k2, g_=g_, col0=col0, cw=cw: h.matmul(dps[k2][:, :cw], lhsT=Bt[:, g_ * 128:(g_ + 1) * 128], rhs=xdts[:, col0:col0 + cw], start=True, stop=True), reads=[bBt, B("xdts")], writes=[B("dps%d" % k2)])
                            op("dve", lambda h, col0=col0, cw=cw, hq0=hq0, nh=nh: h.tensor_tensor(out=HT[:, col0:col0 + cw].rearrange("p (h q) -> p h q", q=64), in0=HT[:, col0:col0 + cw].rearrange("p (h q) -> p h q", q=64), in1=sm[:, 3, hq0:hq0 + nh].unsqueeze(2).to_broadcast([128, nh, 64]), op=ALU.mult), reads=[bs], writes=[bHp])
                            op("dve", lambda h, k2=k2, col0=col0, cw=cw: h.tensor_tensor(out=HT[:, col0:col0 + cw], in0=HT[:, col0:col0 + cw], in1=dps[k2][:, :cw], op=ALU.add), reads=[B("dps%d" % k2)], writes=[bHp])
                            op("act", lambda h, col0=col0, cw=cw: h.copy(out=HTb[:, col0:col0 + cw], in_=HT[:, col0:col0 + cw]), reads=[bHp], writes=[bHb])
                    if final:
                        bq = B("ssqp")
                        op("dve", lambda h: h.reduce_sum(out=ssqp[:, NU:NU + 1], in_=ssqp[:, 0:NU], axis=AX.X), writes=[bq])
                        op("dve", lambda h: h.tensor_scalar(out=ssqp[:, NU + 1:NU + 2], in0=ssqp[:, NU:NU + 1], scalar1=1.0 / DI, scalar2=EPS, op0=ALU.mult, op1=ALU.add), writes=[bq])
                        op("act", lambda h: h.sqrt(out=ssqp[:, NU + 2:NU + 3], in_=ssqp[:, NU + 1:NU + 2]), writes=[bq])
                        op("dve", lambda h: h.reciprocal(out=rstdy[:, gchunk:gchunk + 1], in_=ssqp[:, NU + 2:NU + 3]), reads=[bq], writes=[B("rstdy")])
                        for q0 in range(0, NCD, 4):
                            pt, bp = tps[0], B("stps0")
                            gi_ = (q0 // 16) % 2
                            gs_, bgs = gstg[gi_], B("gstg%d" % gi_)

                            def tr(h, pt=pt, q0=q0):
                                ins = None
                                for q in range(4):
                                    ins = h.transpose(pt[:, q, :], ybf[:, (q0 + q) * 128:(q0 + q + 1) * 128], identb[:])
                                return ins
                            op("pe", tr, reads=[B("ybf"), bC], writes=[bp])
                            for q in range(4):
                                cq = q0 + q
                                if q % 2 == 0:
                                    op("act", lambda h, pt=pt, q=q, cq=cq, gs_=gs_: h.activation(out=gs_[:, cq % 16, :], in_=pt[:, q, :], func=AF.Identity, scale=gnc[:, cq:cq + 1]), reads=[bp, B("gnc")], writes=[bgs])
                                else:
                                    op("dve", lambda h, pt=pt, q=q, cq=cq, gs_=gs_: h.tensor_scalar(out=gs_[:, cq % 16, :], in0=pt[:, q, :], scalar1=gnc[:, cq:cq + 1], scalar2=None, op0=ALU.mult), reads=[bp, B("gnc")], writes=[bgs])
                            if (q0 + 4) % 16 == 0 or q0 + 4 >= NCD:
                                qa = (q0 // 16) * 16
                                qb = min(NCD, qa + 16)
                                load(gT[qa * 128:qb * 128, t0:t0 + 128].rearrange("(k p) t -> p k t", p=128), gs_[:, 0:qb - qa, :], reads=[bgs])

                def init_state(typ, d):
                    if typ == "p":
                        op("dve", lambda h: h.memset(HT[:], 0.0), writes=list(bHT.values()))
                        op("pool", lambda h: h.memset(HTb[:], 0.0), writes=list(bHTb.values()))
                    else:
                        for cq in range(NCD):
                            load(sts[:], st_in[d, cq * 128:(cq + 1) * 128, :], writes=[B("sts")])
                            op("pe", lambda h: h.transpose(fps[:], sts[:], identf[:]), reads=[B("sts"), bC], writes=[B("fps")])
                            op("dve", lambda h, cq=cq: h.tensor_copy(out=HT[:, cq * 128:(cq + 1) * 128], in_=fps[:]), reads=[B("fps")], writes=[bHT[(cq * 128 // pw) * pw]])
                        op("act", lambda h: h.copy(out=HTb[:], in_=HT[:]), reads=list(bHT.values()), writes=list(bHTb.values()))

                def out_state(si, d):
                    for cq in range(NCD):
                        op("pe", lambda h, cq=cq: h.transpose(fps[:], HT[:, cq * 128:(cq + 1) * 128], identf[:]), reads=[bHT[(cq * 128 // pw) * pw], bC], writes=[B("fps")])
                        op("dve", lambda h: h.tensor_copy(out=sts[:], in_=fps[:]), reads=[B("fps")], writes=[B("sts")])
                        load(ns_out[si, d, cq * 128:(cq + 1) * 128, :], sts[:], reads=[B("sts")])

                plan = []
                for (s0, L, typ, si) in seqs:
                    ncn = L // 128
                    plan.append(("init", typ, 1))
                    for cidx in reversed(range(ncn)):
                        plan.append(("chunk", s0, cidx, 1, False, (s0 // 128) + cidx))
                    if typ == "p":
                        plan.append(("out", si, 1))
                    plan.append(("init", typ, 0))
                    for cidx in range(ncn):
                        plan.append(("chunk", s0, cidx, 0, True, (s0 // 128) + cidx))
                    if typ == "p":
                        plan.append(("out", si, 0))
                chunks = [p for p in plan if p[0] == "chunk"]
                ci = 0
                chunk_loads(chunks[0][1], chunks[0][2], chunks[0][3], 0)
                for p in plan:
                    if p[0] == "init":
                        init_state(p[1], p[2])
                    elif p[0] == "out":
                        out_state(p[1], p[2])
                    else:
                        nx = chunks[ci + 1] if ci + 1 < len(chunks) else None
                        chunk(p[1], p[2], p[3], p[4], p[5], ci % 2, (nx[1], nx[2], nx[3]) if nx else None)
                        ci += 1
                sch.barrier()

        def phase_final():
            with ExitStack() as e7:
                fwb = sb("fwb", [128, D], F32, e7)
                hb = [sb("fhb%d" % i, [128, D], F32, e7) for i in range(2)]
                yo = [sb("fyo%d" % i, [128, D], F32, e7) for i in range(2)]
                junk = sb("fjunk", [128, D], BF16, e7)
                ssq = sb("fssq", [128, 4], F32, e7)
                e7.enter_context(nc.Block())
                load(fwb[:], fn_w.partition_broadcast(128), writes=[B("fwb")])
                bq = B("fssq")
                for tt in range(T // 128):
                    i2 = tt % 2
                    load(hb[i2][:], h2[tt * 128:(tt + 1) * 128, :], writes=[B("fhb%d" % i2)])
                    op("dve", lambda h: h.memset(ssq[:], 0.0), writes=[bq])
                    op("act", lambda h, i2=i2: h.activation(out=junk[:], in_=hb[i2][:], func=AF.Square, accum_out=ssq[:, 0:1]), reads=[B("fhb%d" % i2)], writes=[B("fjunk"), bq])
                    op("dve", lambda h: h.tensor_scalar(out=ssq[:, 1:2], in0=ssq[:, 0:1], scalar1=1.0 / D, scalar2=EPS, op0=ALU.mult, op1=ALU.add), writes=[bq])
                    op("act", lambda h: h.sqrt(out=ssq[:, 2:3], in_=ssq[:, 1:2]), writes=[bq])
                    op("dve", lambda h: h.reciprocal(out=ssq[:, 3:4], in_=ssq[:, 2:3]), writes=[bq])
                    op("dve", lambda h, i2=i2: h.scalar_tensor_tensor(out=yo[i2][:], in0=hb[i2][:], scalar=ssq[:, 3:4], in1=fwb[:], op0=ALU.mult, op1=ALU.mult), reads=[B("fhb%d" % i2), bq, B("fwb")], writes=[B("fyo%d" % i2)])
                    load(y_out[tt * 128:(tt + 1) * 128, :], yo[i2][:], reads=[B("fyo%d" % i2)])
                sch.barrier()

        phase_ada()
        phase_norm_inproj(0)
        phase_mid0()
        phase_outproj(0)
        phase_norm_inproj(1)
        phase_conv()
        phase_scan()
        phase_outproj(1)
        phase_final()
    return nc


_NC_CACHE = {}


def run_cfg(cfg, inputs, n_cores=8):
    c = cfg
    key = (c.D, c.NPS, c.SEQ, c.DSEQ)
    if key not in _NC_CACHE:
        _NC_CACHE[key] = build(c)
    nc = _NC_CACHE[key]
    f = lambda a: np.ascontiguousarray(np.asarray(a, dtype=np.float32))
    xp = f(inputs["x_prompt"])
    xs = f(inputs["x_sample"])
    st = f(inputs["state_ssd"])
    cc = f(inputs["c"])
    cctx = f(inputs["c_ctx"])
    nsamp = xs.shape[0]
    shared = {
        "ada_w": f(inputs["ada_w"]), "ada_b": f(inputs["ada_b"]), "norm_w": f(inputs["norm_w"]),
        "pool_in_w": f(inputs["pool_in_w"])[0], "pool_grp_w": f(inputs["pool_grp_w"])[0],
        "pool_grp_b": f(inputs["pool_grp_b"])[0].reshape(-1), "pool_scale": f(inputs["pool_scale"])[0],
        "pool_out_w": f(inputs["pool_out_w"])[0], "ssd_in_w": f(inputs["ssd_in_w"])[0],
        "ssd_conv_w": f(inputs["ssd_conv_w"])[0], "ssd_conv_b": f(inputs["ssd_conv_b"])[0],
        "ssd_dt_bias": f(inputs["ssd_dt_bias"])[0].reshape(-1), "ssd_A_log": f(inputs["ssd_A_log"])[0].reshape(-1),
        "ssd_D": f(inputs["ssd_D"])[0], "ssd_norm_w": f(inputs["ssd_norm_w"])[0],
        "ssd_out_w": f(inputs["ssd_out_w"])[0], "final_norm_w": f(inputs["final_norm_w"]),
    }
    in_maps = []
    for k in range(n_cores):
        sidx = k % nsamp
        m = dict(shared)
        m["x"] = np.concatenate([xp[k * c.NPS:(k + 1) * c.NPS].reshape(c.TP, c.D), xs[sidx]], axis=0)
        m["st"] = st[sidx, 0].reshape(2, c.H * c.P, c.N)
        m["cv"] = np.stack([cctx, cc[sidx]], axis=0)
        in_maps.append(m)
    res = run_bass_kernel_spmd(nc, in_maps, core_ids=list(range(n_cores)))
    rs = res.results
    yp = np.stack([rs[k]["y"][:c.TP].reshape(c.NPS, c.SEQ, c.D) for k in range(n_cores)], axis=0).reshape(n_cores * c.NPS, c.SEQ, c.D)
    ys = np.stack([rs[k]["y"][c.TP:] for k in range(nsamp)], axis=0)
    ns = np.concatenate([rs[k]["ns"] for k in range(n_cores)], axis=0).reshape(n_cores * c.NPS, 1, 2, c.H, c.P, c.N)
    return yp.astype(np.float32), ys.astype(np.float32), ns.astype(np.float32)


def kernel(**inputs):
    return run_cfg(Cfg(), inputs, 8)
```
